# Optimizing a Trainium2 kernel written in Bass

```python
import jax, jax.numpy as jnp
from jax import lax
import numpy as np

D_MODEL = 1024
BATCH = 8
SEQ = 4096
DEPTH = 1

MIX_WIDTH = D_MODEL
POOL_WIDTH = MIX_WIDTH // 2
POOL_WINDOWS = (2, 4, 8, 16)
N_POOL_GROUPS = len(POOL_WINDOWS)
POOL_GROUP_DIM = POOL_WIDTH // N_POOL_GROUPS
RET_WIDTH = MIX_WIDTH - POOL_WIDTH
RET_HEADS = 4
RET_HEAD_DIM = RET_WIDTH // RET_HEADS
RET_CHUNK = 128
ROPE_BASE = 10000.0
IN_WIDTH = POOL_WIDTH + 4 * RET_WIDTH
D_FF = 2816
EPS = 1e-6

kernel_name = "macaron_pool_retention_hybrid"


def rmsnorm(x, gain):
    xf = x.astype(jnp.float32)
    y = xf * lax.rsqrt(jnp.mean(xf * xf, axis=-1, keepdims=True) + EPS)
    return (y * gain.astype(jnp.float32)).astype(x.dtype)


def swiglu(x, w1, w3, w2):
    return (jax.nn.silu(x @ w1) * (x @ w3)) @ w2


def pool_mixer(u, pool_w, pool_scale):
    B, S, _ = u.shape
    ug = u.reshape(B, S, N_POOL_GROUPS, POOL_GROUP_DIM)
    t = jnp.arange(1, S + 1, dtype=jnp.float32)
    outs = []
    for gi, w in enumerate(POOL_WINDOWS):
        xg = ug[:, :, gi, :]
        cs = lax.cumsum(xg.astype(jnp.float32), axis=1)
        csp = jnp.pad(cs, ((0, 0), (w, 0), (0, 0)))
        window_sum = csp[:, w:, :] - csp[:, :S, :]
        count = jnp.minimum(t, float(w))[None, :, None]
        pooled = (window_sum / count).astype(u.dtype) - xg
        outs.append(pooled @ pool_w[gi])
    return jnp.concatenate(outs, axis=-1) * pool_scale


def rope_tables(S, D):
    inv_freq = 1.0 / (ROPE_BASE ** (jnp.arange(0, D, 2, dtype=jnp.float32) / D))
    ang = jnp.arange(S, dtype=jnp.float32)[:, None] * inv_freq[None, :]
    return jnp.cos(ang), jnp.sin(ang)


def apply_rope(t, cos, sin):
    t1, t2 = jnp.split(t, 2, axis=-1)
    c = cos[None, :, None, :]
    s = sin[None, :, None, :]
    return jnp.concatenate([t1 * c - t2 * s, t1 * s + t2 * c], axis=-1)


def retention(q, k, v, g, gain):
    B, S, _ = q.shape
    H, D, C = RET_HEADS, RET_HEAD_DIM, RET_CHUNK
    NC = S // C
    dt = q.dtype

    def heads(t):
        return t.astype(jnp.float32).reshape(B, S, H, D)

    cos, sin = rope_tables(S, D)
    qh = apply_rope(heads(q), cos, sin)
    kh = apply_rope(heads(k), cos, sin) * (D ** -0.5)
    vh = heads(v)

    def chunks(t):
        return t.reshape(B, NC, C, H, D).transpose(0, 3, 1, 2, 4)

    qc, kc, vc = chunks(qh), chunks(kh), chunks(vh)

    log_gamma = jnp.log1p(-jnp.exp2(-5.0 - jnp.arange(H, dtype=jnp.float32)))
    pos = jnp.arange(C, dtype=jnp.float32)
    rel = pos[:, None] - pos[None, :]
    intra_decay = jnp.where(rel[None] >= 0,
                            jnp.exp(log_gamma[:, None, None] * jnp.maximum(rel, 0.0)[None]),
                            0.0)

    scores = jnp.einsum('bhncd,bhnsd->bhncs', qc, kc) * intra_decay[None, :, None]
    o_intra = jnp.einsum('bhncs,bhnse->bhnce', scores, vc)

    k_tail = jnp.exp(log_gamma[:, None] * (C - 1 - pos)[None, :])
    kv = jnp.einsum('bhnsd,bhnse->nbhde', kc * k_tail[None, :, None, :, None], vc)
    chunk_decay = jnp.exp(log_gamma * C)[None, :, None, None]

    def step(R, kv_n):
        return chunk_decay * R + kv_n, R

    _, R_prev = lax.scan(step, jnp.zeros((B, H, D, D), jnp.float32), kv)

    q_head = jnp.exp(log_gamma[:, None] * (pos + 1.0)[None, :])
    o_cross = jnp.einsum('bhncd,nbhde->bhnce', qc * q_head[None, :, None, :, None], R_prev)

    o = (o_intra + o_cross).transpose(0, 2, 3, 1, 4).reshape(B, S, H, D)
    o = o * lax.rsqrt(jnp.mean(o * o, axis=-1, keepdims=True) + EPS)
    o = o.reshape(B, S, H * D) * gain.astype(jnp.float32)
    return (jax.nn.silu(g.astype(jnp.float32)) * o).astype(dt)


def setup_inputs(seed: int = 0) -> dict:
    key = jax.random.key(seed)
    ks = jax.random.split(key, 20)
    f32 = jnp.float32

    def nrm(k, shape, fan_in):
        return jax.random.normal(k, shape, f32) * (fan_in ** -0.5)

    def gain(k, shape):
        return 1.0 + 0.05 * jax.random.normal(k, shape, f32)

    L = DEPTH
    return {
        "x": jax.random.normal(ks[0], (BATCH, SEQ, D_MODEL), f32),
        "ffn1_norm": gain(ks[1], (L, D_MODEL)),
        "ffn1_w1": nrm(ks[2], (L, D_MODEL, D_FF), D_MODEL),
        "ffn1_w3": nrm(ks[3], (L, D_MODEL, D_FF), D_MODEL),
        "ffn1_w2": nrm(ks[4], (L, D_FF, D_MODEL), D_FF),
        "mix_norm": gain(ks[5], (L, D_MODEL)),
        "w_in": nrm(ks[6], (L, D_MODEL, IN_WIDTH), D_MODEL),
        "pool_w": nrm(ks[7], (L, N_POOL_GROUPS, POOL_GROUP_DIM, POOL_GROUP_DIM), POOL_GROUP_DIM),
        "pool_scale": gain(ks[8], (L, POOL_WIDTH)),
        "ret_norm": gain(ks[9], (L, RET_WIDTH)),
        "w_out": nrm(ks[10], (L, MIX_WIDTH, D_MODEL), MIX_WIDTH),
        "ffn2_norm": gain(ks[11], (L, D_MODEL)),
        "ffn2_w1": nrm(ks[12], (L, D_MODEL, D_FF), D_MODEL),
        "ffn2_w3": nrm(ks[13], (L, D_MODEL, D_FF), D_MODEL),
        "ffn2_w2": nrm(ks[14], (L, D_FF, D_MODEL), D_FF),
        "final_norm": gain(ks[15], (D_MODEL,)),
    }


def reference(x, ffn1_norm, ffn1_w1, ffn1_w3, ffn1_w2, mix_norm, w_in, pool_w, pool_scale,
              ret_norm, w_out, ffn2_norm, ffn2_w1, ffn2_w3, ffn2_w2, final_norm):
    h = x
    for l in range(DEPTH):
        h = h + 0.5 * swiglu(rmsnorm(h, ffn1_norm[l]), ffn1_w1[l], ffn1_w3[l], ffn1_w2[l])

        u = rmsnorm(h, mix_norm[l])
        proj = u @ w_in[l]
        u_pool, q, k, v, g = jnp.split(
            proj, [POOL_WIDTH, POOL_WIDTH + RET_WIDTH, POOL_WIDTH + 2 * RET_WIDTH,
                   POOL_WIDTH + 3 * RET_WIDTH], axis=-1)
        a = pool_mixer(u_pool, pool_w[l], pool_scale[l])
        b = retention(q, k, v, g, ret_norm[l])
        h = h + jnp.concatenate([a, b], axis=-1) @ w_out[l]

        h = h + 0.5 * swiglu(rmsnorm(h, ffn2_norm[l]), ffn2_w1[l], ffn2_w3[l], ffn2_w2[l])
    return rmsnorm(h, final_norm)
```

```python
import numpy as np
from contextlib import ExitStack
import concourse.bass as bass
import concourse.mybir as mybir
from concourse.bass_utils import run_bass_kernel_spmd

F32 = mybir.dt.float32
BF = mybir.dt.bfloat16
AF = mybir.ActivationFunctionType
ALU = mybir.AluOpType

D = 1024
S = 4096
DFF = 2816
T = 512
NTT = 4
KC = 8
NF = 22
NG = 11
H = 4
C = 128
EPS = 1e-6
NSTREAM = 80
NS = 16
ENGS = ("pe", "act", "dve", "pool", "sp")
WARM = [8, 8, 0, 8]
CMERGE = 0
ORDER = 1


class Buf:
    __slots__ = ("w", "r", "rd", "name", "const")

    def __init__(self, name="", const=False):
        self.w = None
        self.r = {}
        self.rd = []
        self.name = name
        self.const = const


class Op:
    __slots__ = ("eng", "fn", "deps", "needed", "val", "sem", "ndma", "seq")

    def __init__(self, eng, fn):
        self.eng = eng
        self.fn = fn
        self.deps = []
        self.needed = False
        self.val = None
        self.sem = None
        self.ndma = 0
        self.seq = 0


class Sched:
    def __init__(self):
        self.ops = {e: [] for e in ENGS}
        self.dma_cum = {}

    def _track(self, o, reads, writes):
        o.seq = len(self.ops[o.eng])
        cand = []
        for b in reads:
            if b.w is not None:
                cand.append(b.w)
        for b in writes:
            if b.w is not None:
                cand.append(b.w)
            cand.extend(b.r.values())
            cand.extend(b.rd)
        best = {}
        for d in cand:
            if d is o:
                continue
            if d.ndma:
                key = ("dma", d.sem.num)
                if key not in best or best[key].val < d.val:
                    best[key] = d
            else:
                if d.eng == "pe" and o.eng == "pe":
                    continue
                key = ("eng", d.eng)
                if key not in best or best[key].seq < d.seq:
                    best[key] = d
        for d in best.values():
            o.deps.append(d)
            d.needed = True
        for b in reads:
            if b.const:
                continue
            if o.ndma:
                b.rd.append(o)
            else:
                b.r[o.eng] = o
        for b in writes:
            b.w = o
            b.r = {}
            b.rd = []

    def op(self, eng, fn, reads=(), writes=()):
        o = Op(eng, fn)
        self._track(o, reads, writes)
        self.ops[eng].append(o)
        return o

    def dma(self, eng, fn, sem, n, reads=(), writes=()):
        o = Op(eng, fn)
        o.ndma = n
        o.sem = sem
        key = sem.num
        self.dma_cum[key] = self.dma_cum.get(key, 0) + 16 * n
        o.val = self.dma_cum[key]
        self._track(o, reads, writes)
        self.ops[eng].append(o)
        return o

    def wait_only(self, eng, deps):
        o = Op(eng, None)
        for d in deps:
            o.deps.append(d)
            d.needed = True
        self.ops[eng].append(o)
        return o

    def finalize(self, engsem):
        for e in ENGS:
            cnt = 0
            for o in self.ops[e]:
                if o.ndma:
                    continue
                if o.needed and o.fn is not None:
                    cnt += 1
                    o.val = cnt
                    o.sem = engsem[e]

    def emit(self, eng, h):
        waited = {}
        for o in self.ops[eng]:
            need = {}
            for d in o.deps:
                k = d.sem.num
                if waited.get(k, 0) >= d.val:
                    continue
                if k not in need or need[k][1] < d.val:
                    need[k] = (d.sem, d.val)
            for k, (sem, val) in need.items():
                h.wait_ge(sem, val)
                waited[k] = val
            if o.fn is None:
                continue
            r = o.fn(h)
            if o.ndma:
                assert len(r) == o.ndma
                for ins in r:
                    ins.then_inc(o.sem, 16)
            elif o.needed:
                r.then_inc(o.sem, 1)


def build_nc(ntiles=S // T):
    nc = bass.Bass("TRN2", target_bir_lowering=False)
    ntok = ntiles * T

    def din(name, shape, dt=F32):
        return nc.dram_tensor(name, list(shape), dt, kind="ExternalInput").ap()

    x = din("x", [ntok, D])
    y = nc.dram_tensor("y", [ntok, D], F32, kind="ExternalOutput").ap()
    w1 = [din("w1a", [D, DFF]), din("w1b", [D, DFF])]
    w3 = [din("w3a", [D, DFF]), din("w3b", [D, DFF])]
    w2 = [din("w2a", [DFF, D]), din("w2b", [DFF, D])]
    w_in = din("w_in", [D, 2560])
    w_out = din("w_out", [D, D])
    pool_w = din("pool_w", [4, 128, 128])
    gains = din("gains", [3, D])
    fgain = din("fgain", [D])
    retg = din("retg", [512])
    pscale = din("pscale", [128, 4])
    c_ident = din("c_ident", [128, 128])
    c_bands = din("c_bands", [128, 12, 128])
    c_mask = din("c_mask", [128, 128])
    c_fac = din("c_fac", [128, 8])
    c_gc = din("c_gc", [128, 512])
    c_rope = din("c_rope", [S, 128])
    scr = nc.dram_tensor("scr", [NSTREAM, 128, 2048], BF, kind="Internal").ap()

    es = ExitStack()
    with es:
        def sb(name, shape, dt):
            return es.enter_context(nc.sbuf_tensor(name, list(shape), dt))

        def sem(name):
            return es.enter_context(nc.semaphore(name))

        hbuf = [sb(f"h{i}", [128, NTT, D], F32) for i in range(2)]
        xnb = sb("xnb", [128, NTT, D], BF)
        xnT = sb("xnT", [128, KC, T], BF)
        actT = sb("actT", [128, NF, T], BF)
        ring_all = sb("ring_all", [128, NS, 2048], BF)
        ring = [ring_all[:, i, :] for i in range(NS)]
        silu_t = [sb(f"silu{i}", [128, T], F32) for i in range(2)]
        gain_cur = sb("gain_cur", [128, D], F32)
        fgain_b = sb("fgain_b", [128, D], F32)
        retg_b = sb("retg_b", [128, 512], F32)
        pscale_t = sb("pscale_t", [128, 4], F32)
        rope_t = [sb(f"rope{i}", [128, NTT, 128], F32) for i in range(2)]
        ident = sb("ident", [128, 128], BF)
        bands = sb("bands", [128, 12, 128], BF)
        mask_t = sb("mask_t", [128, 128], F32)
        fac_t = sb("fac_t", [128, 8], F32)
        gc_t = sb("gc_t", [128, 512], F32)
        eps_t = sb("eps_t", [128, 1], F32)
        poolw_t = sb("poolw_t", [128, 4, 128], BF)
        ssq = sb("ssq", [128, 8], F32)
        rt = sb("rt", [128, 8], F32)
        rstd = sb("rstd", [128, 8], F32)
        up_tok = sb("up_tok", [128, 5, 512], BF)
        v_tok = sb("v_tok", [128, 4, 512], BF)
        sg32 = sb("sg32", [128, 512], F32)
        sgb = sb("sgb", [128, 4, 512], BF)
        rB = sb("rB", [128, 2, 512], F32)
        qrot = sb("qrot", [128, 2, 512], BF)
        krot = sb("krot", [128, 2, 512], BF)
        qkT = sb("qkT", [128, 4, 2, 512], BF)
        PT = sb("PT", [128, 2, 512], BF)
        R32 = sb("R32", [128, 512], F32)
        Rbf = sb("Rbf", [128, 5, 512], BF)
        btok = sb("btok", [128, 4, 512], BF)
        ossq = sb("ossq", [128, 2, 4], F32)
        ort = sb("ort", [128, 2, 4], F32)
        orinv = sb("orinv", [128, 2, 4], F32)
        junk = rB[:].rearrange("p a b -> p (a b)").bitcast(BF)[:, 0:D]
        mixT = actT
        f32v = actT[:, 12:22, :].rearrange("p a b -> p (a b)").bitcast(F32)
        qs32 = [f32v[:, 0:512], f32v[:, 512:1024]]
        ks32 = [f32v[:, 1024:1536], f32v[:, 1536:2048]]
        tmpKV = f32v[:, 2048:2560]

        psum = [es.enter_context(nc.psum_tensor(f"ps{i}", [128, 512], F32)) for i in range(8)]

        engsem = {e: sem(f"sem_{e}") for e in ("pe", "act", "dve", "pool")}
        ring_sem = [sem(f"ring{i}") for i in range(NS)]
        ring_sw_sem = [sem(f"ringsw{i}") for i in range(NS)]
        fill_sem = [sem(f"fill{i}") for i in range(NSTREAM // 2)]
        x_sem = [sem("x0"), sem("x1")]
        o_sem = [sem("o0"), sem("o1")]
        rope_sem = [sem("rp0"), sem("rp1")]
        c_sem = sem("consts")
        pw_sem = sem("poolw")
        g_sem = sem("gain")

        sc = Sched()

        B_h = [[Buf(f"h{b}_{t}") for t in range(NTT)] for b in range(2)]
        B_xnb = [Buf() for _ in range(NTT)]
        B_xnT = [Buf() for _ in range(NTT)]
        B_act = [Buf(f"act{f}") for f in range(NF)]
        B_ring = [Buf(f"ring{i}") for i in range(NS)]
        B_scr = [Buf() for _ in range(NSTREAM)]
        B_P = [Buf(f"P{i}") for i in range(8)]
        B_silu = [Buf(), Buf()]
        B_const = Buf("const", const=True)
        B_cbf = Buf("cbf", const=True)
        B_gain = Buf("gain")
        B_rope = [Buf(), Buf()]
        B_ssq = [Buf() for _ in range(8)]
        B_rt = [Buf() for _ in range(8)]
        B_rstd = [Buf() for _ in range(8)]
        B_up = [Buf() for _ in range(5)]
        B_v = [Buf() for _ in range(4)]
        B_sg32 = Buf()
        B_sgb = [Buf() for _ in range(4)]
        B_rB = [Buf(), Buf()]
        B_qrot = [Buf(), Buf()]
        B_krot = [Buf(), Buf()]
        B_qkT = [Buf() for _ in range(4)]
        B_PT = [Buf(), Buf()]
        B_R32 = Buf()
        B_Rbf = [Buf() for _ in range(5)]
        B_btok = [Buf() for _ in range(4)]
        B_ossq = [[Buf() for _ in range(4)] for _ in range(2)]
        B_ort = [Buf(), Buf()]
        B_orinv = [Buf(), Buf()]
        Bq32 = [B_act[12 + 0], B_act[12 + 1]], [B_act[12 + 2], B_act[12 + 3]]
        Bk32 = [B_act[12 + 4], B_act[12 + 5]], [B_act[12 + 6], B_act[12 + 7]]
        BtKV = [B_act[20], B_act[21]]

        def stream_src(k):
            def blk(w, half, G):
                return w[half * 512:(half + 1) * 512, G * 512:(G + 1) * 512].rearrange("(k p) f -> p k f", p=128)

            def colslot(w, c0):
                return w[:, c0:c0 + 256].rearrange("(k p) f -> p k f", p=128)

            def rowslot(w, j):
                return w[j * 256:(j + 1) * 256, :].rearrange("(a p) d -> p a d", p=128)

            if k < 33:
                ff, kk = 0, k
            elif k < 47:
                kk = k - 33
                if kk < 10:
                    return blk(w_in, kk % 2, kk // 2)
                kk -= 10
                return blk(w_out, kk % 2, kk // 2)
            else:
                ff, kk = 1, k - 47
            if kk < 20:
                G, r = kk // 4, kk % 4
                return blk(w1[ff] if r < 2 else w3[ff], r % 2, G)
            if kk < 22:
                return colslot(w1[ff] if kk == 20 else w3[ff], 2560)
            return rowslot(w2[ff], kk - 22)

        def const_loads(e):
            r = []
            r.append(e.dma_start(out=mask_t[:], in_=c_mask))
            r.append(e.dma_start(out=fac_t[:], in_=c_fac))
            r.append(e.dma_start(out=gc_t[:], in_=c_gc))
            r.append(e.dma_start(out=pscale_t[:], in_=pscale))
            r.append(e.dma_start(out=fgain_b[:], in_=fgain.partition_broadcast(128)))
            r.append(e.dma_start(out=retg_b[:], in_=retg.partition_broadcast(128)))
            return r

        sc.dma("sp", const_loads, c_sem, 6, writes=[B_const])

        def cast_consts(e):
            return [e.dma_start(out=ident[:], in_=c_ident),
                    e.dma_start(out=bands[:], in_=c_bands),
                    e.dma_start(out=poolw_t[:], in_=pool_w.rearrange("g c e -> c g e"))]

        sc.dma("pool", cast_consts, pw_sem, 3, writes=[B_cbf])
        B_eps = Buf("eps", const=True)
        sc.op("dve", lambda e: e.memset(eps_t[:], EPS), writes=[B_eps])
        DIRECT0 = 1
        if not DIRECT0:
            for k in range(NSTREAM):
                def fill(e, k=k):
                    src = stream_src(k)
                    dst = scr[k].rearrange("p (a b) -> p a b", a=src.shape[1])
                    return [e.dma_start(out=dst, in_=src)]
                sc.dma("pool", fill, fill_sem[k // 2], 1, writes=[B_scr[k]])
            for k in range(0, NSTREAM, 2):
                B_scr[k].w = B_scr[k + 1].w
        sc.op("pool", lambda e: e.memset(R32[:], 0.0), writes=[B_R32])
        sc.op("pool", lambda e: e.memset(Rbf[:, 0, :], 0.0), writes=[B_Rbf[0]])

        state = {"next_load": 0, "total": ntiles * NSTREAM}

        def issue_load():
            n = state["next_load"]
            if n >= state["total"]:
                return
            state["next_load"] = n + 1
            s = n % NS
            k = n % NSTREAM
            SPREAD = 4
            tq = n // NSTREAM
            kq = k // (NSTREAM // SPREAD)
            if DIRECT0 and kq >= tq:
                src = stream_src(k)
                dst = ring[s].rearrange("p (a b) -> p a b", a=src.shape[1])
                sc.dma("pool", lambda e: [e.dma_start(out=dst, in_=src)], ring_sw_sem[s], 1, writes=[B_ring[s]])
                if kq == tq:
                    so = sc.dma("sp", lambda e: [e.dma_start(out=scr[k], in_=ring[s])], fill_sem[k // 2], 1,
                                reads=[B_ring[s]], writes=[B_scr[k]])
                    if k % 2 == 0:
                        state["even_store"] = so
                    else:
                        state["even_store"].val = so.val
                return
            sc.dma("sp", lambda e: [e.dma_start(out=ring[s], in_=scr[k])], ring_sem[s], 1,
                   reads=[B_scr[k]], writes=[B_ring[s]])

        def slot_of(ti, k):
            return (ti * NSTREAM + k) % NS

        def load_x(ti):
            b = ti % 2
            src = x[ti * T:(ti + 1) * T, :].rearrange("(t p) d -> p t d", p=128)
            sc.dma("sp", lambda e: [e.dma_start(out=hbuf[b][:], in_=src)], x_sem[b], 1, writes=B_h[b])

        def load_rope(ti):
            b = ti % 2
            src = c_rope[ti * T:(ti + 1) * T, :].rearrange("(t p) d -> p t d", p=128)
            sc.dma("sp", lambda e: [e.dma_start(out=rope_t[b][:], in_=src)], rope_sem[b], 1, writes=[B_rope[b]])

        def load_gain(gi):
            sc.dma("sp", lambda e: [e.dma_start(out=gain_cur[:], in_=gains[gi].partition_broadcast(128))],
                   g_sem, 1, writes=[B_gain])

        out_ops = []

        def store_y(ti, tt):
            b = ti % 2
            dst = y[ti * T + tt * 128: ti * T + (tt + 1) * 128, :]
            out_ops.append(sc.dma("pool", lambda e: [e.dma_start(out=dst, in_=hbuf[b][:, tt, :])], o_sem[b], 1,
                                  reads=[B_h[b][tt]]))

        def norm_tt(hb, tt):
            h = hbuf[hb]
            sc.op("act", lambda e: e.activation(out=xnb[:, tt, :], in_=h[:, tt, :], func=AF.Square,
                                                accum_out=ssq[:, tt:tt + 1]),
                  reads=[B_h[hb][tt]], writes=[B_xnb[tt], B_ssq[tt]])
            sc.op("act", lambda e: e.activation(out=rt[:, tt:tt + 1], in_=ssq[:, tt:tt + 1], func=AF.Sqrt,
                                                bias=eps_t[:], scale=1.0 / D),
                  reads=[B_ssq[tt], B_eps], writes=[B_rt[tt]])
            sc.op("dve", lambda e: e.reciprocal(out=rstd[:, tt:tt + 1], in_=rt[:, tt:tt + 1]),
                  reads=[B_rt[tt]], writes=[B_rstd[tt]])
            sc.op("dve", lambda e: e.scalar_tensor_tensor(
                out=xnb[:, tt, :], in0=h[:, tt, :], scalar=rstd[:, tt:tt + 1], in1=gain_cur[:],
                op0=ALU.mult, op1=ALU.mult),
                reads=[B_h[hb][tt], B_rstd[tt], B_gain], writes=[B_xnb[tt]])

        def tr_tt(tt, bank=None):
            if bank is None:
                bank = tt % 2
            pbf = psum[bank][:].bitcast(BF)
            for kc in range(KC):
                sc.op("pe", lambda e, kc=kc: e.transpose(
                    out=pbf[:, kc * 128:(kc + 1) * 128], in_=xnb[:, tt, kc * 128:(kc + 1) * 128], identity=ident[:]),
                    reads=[B_xnb[tt], B_cbf], writes=[B_P[bank]])
            src = pbf.rearrange("p (k t) -> p k t", k=KC)
            dst = xnT[:, :, tt * 128:(tt + 1) * 128]
            if tt % 2 == 0:
                sc.op("act", lambda e: e.activation(out=dst, in_=src, func=AF.Copy),
                      reads=[B_P[bank]], writes=[B_xnT[tt]])
            else:
                sc.op("dve", lambda e: e.tensor_copy(out=dst, in_=src),
                      reads=[B_P[bank]], writes=[B_xnT[tt]])

        def ffn_phase_a(ti, kbase):
            def fchunk(f, j, w1s, w3s, nk):
                pa, pb = (0, 1) if f % 2 == 0 else (2, 3)
                for (ws, pbk) in ((w1s, pa), (w3s, pb)):
                    for kc in range(KC):
                        sl, wv = ws[kc // nk]
                        sc.op("pe", lambda e, wv=wv, kc=kc, pbk=pbk: e.matmul(
                            out=psum[pbk][:], lhsT=wv[:, kc % nk, j * 128:(j + 1) * 128], rhs=xnT[:, kc, :],
                            start=(kc == 0), stop=(kc == KC - 1)),
                            reads=[B_ring[sl]] + B_xnT, writes=[B_P[pbk]])
                st = f % 2
                sc.op("act", lambda e: e.activation(out=silu_t[st][:], in_=psum[pa][:], func=AF.Silu),
                      reads=[B_P[pa]], writes=[B_silu[st]])
                sc.op("dve", lambda e: e.tensor_tensor(
                    out=actT[:, f, :], in0=silu_t[st][:], in1=psum[pb][:], op=ALU.mult),
                    reads=[B_silu[st], B_P[pb]], writes=[B_act[f]])

            for G in range(5):
                sl = [slot_of(ti, kbase + 4 * G + i) for i in range(4)]
                vw = [(x, ring[x].rearrange("p (k f) -> p k f", k=4)) for x in sl]
                for j in range(4):
                    fchunk(4 * G + j, j, vw[0:2], vw[2:4], 4)
                for _ in range(4):
                    issue_load()
            s1 = slot_of(ti, kbase + 20)
            s3 = slot_of(ti, kbase + 21)
            v1 = [(s1, ring[s1].rearrange("p (k f) -> p k f", k=KC))]
            v3 = [(s3, ring[s3].rearrange("p (k f) -> p k f", k=KC))]
            for j in range(2):
                fchunk(20 + j, j, v1, v3, 8)
            issue_load()
            issue_load()

        def ffn_phase_b_tt(ti, hb, kbase, tt):
            h = hbuf[hb]
            for dh in range(2):
                bank = 4 + (tt % 2) * 2 + dh
                for f in range(NF):
                    sl = slot_of(ti, kbase + 22 + f // 2)
                    wv = ring[sl].rearrange("p (a d) -> p a d", a=2)
                    sc.op("pe", lambda e, wv=wv, f=f, dh=dh, bank=bank: e.matmul(
                        out=psum[bank][:], lhsT=actT[:, f, tt * 128:(tt + 1) * 128],
                        rhs=wv[:, f % 2, dh * 512:(dh + 1) * 512], start=(f == 0), stop=(f == NF - 1)),
                        reads=[B_ring[sl], B_act[f]], writes=[B_P[bank]])
                sc.op("dve", lambda e, dh=dh, bank=bank: e.scalar_tensor_tensor(
                    out=h[:, tt, dh * 512:(dh + 1) * 512], in0=psum[bank][:], scalar=0.5,
                    in1=h[:, tt, dh * 512:(dh + 1) * 512], op0=ALU.mult, op1=ALU.add),
                    reads=[B_P[bank], B_h[hb][tt]], writes=[B_h[hb][tt]])

        def mixer(ti, hb, tr3_pending):
            h = hbuf[hb]
            rp = rope_t[ti % 2]
            B_rp = B_rope[ti % 2]
            kbase = 33
            win_slots = [slot_of(ti, kbase + i) for i in range(10)]
            wout_slots = [slot_of(ti, kbase + 10 + i) for i in range(4)]
            gn = [ti * 4 + c for c in range(4)]

            def rope(eng, X32, Bsrc, ri, dst, Bdst, c):
                X = X32.rearrange("p (h t d) -> p h t d", h=H, t=2)
                Bv = rB[:, ri, :].rearrange("p (h t d) -> p h t d", h=H, t=2)
                O = dst.rearrange("p (h t d) -> p h t d", h=H, t=2)
                cos4 = rp[:, c, 0:64].unsqueeze(1).unsqueeze(1).broadcast_to([128, H, 2, 64])
                sin3 = rp[:, c, 64:128].unsqueeze(1).broadcast_to([128, H, 64])
                sc.op(eng, lambda e: e.tensor_tensor(out=Bv[:, :, 0, :], in0=X[:, :, 1, :], in1=sin3, op=ALU.mult),
                      reads=Bsrc + [B_rp], writes=[B_rB[ri]])
                sc.op(eng, lambda e: e.tensor_tensor(out=Bv[:, :, 1, :], in0=X[:, :, 0, :], in1=sin3, op=ALU.mult),
                      reads=Bsrc + [B_rp], writes=[B_rB[ri]])
                sc.op(eng, lambda e: e.tensor_tensor(out=X, in0=X, in1=cos4, op=ALU.mult),
                      reads=Bsrc + [B_rp], writes=Bsrc)
                sc.op(eng, lambda e: e.tensor_tensor(out=O[:, :, 0, :], in0=X[:, :, 0, :], in1=Bv[:, :, 0, :],
                                                     op=ALU.subtract),
                      reads=Bsrc + [B_rB[ri]], writes=[Bdst])
                sc.op(eng, lambda e: e.tensor_tensor(out=O[:, :, 1, :], in0=X[:, :, 1, :], in1=Bv[:, :, 1, :],
                                                     op=ALU.add),
                      reads=Bsrc + [B_rB[ri]], writes=[Bdst])

            def proj(c):
                n = gn[c]
                p2 = c % 2
                for bi in range(5):
                    for kc in range(KC):
                        sl = win_slots[bi * 2 + kc // 4]
                        wv = ring[sl].rearrange("p (k f) -> p k f", k=4)
                        sc.op("pe", lambda e, kc=kc, bi=bi, wv=wv: e.matmul(
                            out=psum[bi][:], lhsT=xnT[:, kc, c * 128:(c + 1) * 128], rhs=wv[:, kc % 4, :],
                            start=(kc == 0), stop=(kc == KC - 1)),
                            reads=[B_ring[sl], B_xnT[c]], writes=[B_P[bi]])
                    if bi == 0:
                        sc.op("act", lambda e: e.activation(out=up_tok[:, n % 5, :], in_=psum[0][:], func=AF.Copy),
                              reads=[B_P[0]], writes=[B_up[n % 5]])
                    elif bi == 1:
                        for hh in range(H):
                            sc.op("act", lambda e, hh=hh: e.activation(
                                out=qs32[p2][:, hh * 128:(hh + 1) * 128], in_=psum[1][:, hh * 128:(hh + 1) * 128],
                                func=AF.Copy, scale=fac_t[:, hh:hh + 1]),
                                reads=[B_P[1], B_const], writes=[Bq32[p2][hh // 2]])
                        rope("pool", qs32[p2], Bq32[p2], 0, qrot[:, p2, :], B_qrot[p2], c)
                    elif bi == 2:
                        for hh in range(H):
                            sc.op("act", lambda e, hh=hh: e.activation(
                                out=ks32[p2][:, hh * 128:(hh + 1) * 128], in_=psum[2][:, hh * 128:(hh + 1) * 128],
                                func=AF.Copy, scale=fac_t[:, 4 + hh:5 + hh]),
                                reads=[B_P[2], B_const], writes=[Bk32[p2][hh // 2]])
                        rope("dve", ks32[p2], Bk32[p2], 1, krot[:, p2, :], B_krot[p2], c)
                    elif bi == 3:
                        sc.op("act", lambda e: e.activation(out=v_tok[:, c, :], in_=psum[3][:], func=AF.Copy),
                              reads=[B_P[3]], writes=[B_v[c]])
                    else:
                        sc.op("act", lambda e: e.activation(out=sg32[:], in_=psum[4][:], func=AF.Silu),
                              reads=[B_P[4]], writes=[B_sg32])
                        sc.op("pool", lambda e: e.tensor_tensor(out=sgb[:, c, :], in0=sg32[:], in1=retg_b[:],
                                                                op=ALU.mult),
                              reads=[B_sg32, B_const], writes=[B_sgb[c]])

            def stage_a_tr(c):
                n = gn[c]
                p2 = c % 2
                pbf = psum[5][:].bitcast(BF)
                for qi, (srcT, Bs) in enumerate(((qrot, B_qrot[p2]), (krot, B_krot[p2]))):
                    for hh in range(H):
                        sc.op("pe", lambda e, qi=qi, hh=hh, srcT=srcT: e.transpose(
                            out=pbf[:, qi * 512 + hh * 128: qi * 512 + (hh + 1) * 128],
                            in_=srcT[:, p2, hh * 128:(hh + 1) * 128], identity=ident[:]),
                            reads=[Bs, B_cbf], writes=[B_P[5]])
                sc.op("dve", lambda e: e.tensor_copy(out=qkT[:, c, :, :].rearrange("p a b -> p (a b)"), in_=pbf),
                      reads=[B_P[5]], writes=[B_qkT[c]])

            def stage_a_kv(c, kvb=6):
                n = gn[c]
                p2 = c % 2
                for hh in range(H):
                    sc.op("pe", lambda e, hh=hh: e.matmul(
                        out=psum[kvb][:, hh * 128:(hh + 1) * 128], lhsT=krot[:, p2, hh * 128:(hh + 1) * 128],
                        rhs=v_tok[:, c, hh * 128:(hh + 1) * 128], start=True, stop=True),
                        reads=[B_krot[p2], B_v[c]], writes=[B_P[kvb]])
                sc.op("dve", lambda e: e.tensor_tensor(out=tmpKV, in0=psum[kvb][:], in1=R32[:], op=ALU.add),
                      reads=[B_P[kvb], B_R32], writes=BtKV)
                sc.op("dve", lambda e: e.tensor_tensor(out=R32[:], in0=tmpKV, in1=gc_t[:], op=ALU.mult),
                      reads=BtKV + [B_const], writes=[B_R32])
                sc.op("act", lambda e: e.activation(out=Rbf[:, (n + 1) % 5, :], in_=R32[:], func=AF.Copy),
                      reads=[B_R32], writes=[B_Rbf[(n + 1) % 5]])


            def stage_a(c):
                stage_a_tr(c)
                stage_a_kv(c)

            def stage_b(c, bank=None):
                if bank is None:
                    bank = c % 2
                p2 = c % 2
                for hh in range(H):
                    sc.op("pe", lambda e, hh=hh: e.matmul(
                        out=psum[bank][:, hh * 128:(hh + 1) * 128], lhsT=qkT[:, c, 1, hh * 128:(hh + 1) * 128],
                        rhs=qkT[:, c, 0, hh * 128:(hh + 1) * 128], start=True, stop=True),
                        reads=[B_qkT[c]], writes=[B_P[bank]])
                m4 = mask_t[:].unsqueeze(1).broadcast_to([128, H, 128])
                sc.op("dve", lambda e: e.tensor_tensor(
                    out=PT[:, p2, :].rearrange("p (h c) -> p h c", h=H),
                    in0=psum[bank][:].rearrange("p (h c) -> p h c", h=H), in1=m4, op=ALU.mult),
                    reads=[B_P[bank], B_const], writes=[B_PT[p2]])

            def stage_c_old(c, bank=None):
                n = gn[c]
                if bank is None:
                    bank = 2 + c % 2
                p2 = c % 2
                for hh in range(H):
                    sc.op("pe", lambda e, hh=hh: e.matmul(
                        out=psum[bank][:, hh * 128:(hh + 1) * 128], lhsT=PT[:, p2, hh * 128:(hh + 1) * 128],
                        rhs=v_tok[:, c, hh * 128:(hh + 1) * 128], start=True, stop=False),
                        reads=[B_PT[p2], B_v[c]], writes=[B_P[bank]])
                    sc.op("pe", lambda e, hh=hh: e.matmul(
                        out=psum[bank][:, hh * 128:(hh + 1) * 128], lhsT=qkT[:, c, 0, hh * 128:(hh + 1) * 128],
                        rhs=Rbf[:, n % 5, hh * 128:(hh + 1) * 128], start=False, stop=True),
                        reads=[B_qkT[c], B_Rbf[n % 5]], writes=[B_P[bank]])
                for hh in range(H):
                    sc.op("act", lambda e, hh=hh: e.activation(
                        out=junk[:, hh * 128:(hh + 1) * 128], in_=psum[bank][:, hh * 128:(hh + 1) * 128],
                        func=AF.Square, accum_out=ossq[:, p2, hh:hh + 1]),
                        reads=[B_P[bank]], writes=[B_ossq[p2][hh]])
                sc.op("act", lambda e: e.activation(out=ort[:, p2, :], in_=ossq[:, p2, :], func=AF.Sqrt, bias=eps_t[:],
                                                    scale=1.0 / 128),
                      reads=B_ossq[p2] + [B_eps], writes=[B_ort[p2]])
                sc.op("dve", lambda e: e.reciprocal(out=orinv[:, p2, :], in_=ort[:, p2, :]),
                      reads=[B_ort[p2]], writes=[B_orinv[p2]])
                for hh in range(H):
                    sc.op("dve", lambda e, hh=hh: e.scalar_tensor_tensor(
                        out=btok[:, c, hh * 128:(hh + 1) * 128], in0=psum[bank][:, hh * 128:(hh + 1) * 128],
                        scalar=orinv[:, p2, hh:hh + 1], in1=sgb[:, c, hh * 128:(hh + 1) * 128],
                        op0=ALU.mult, op1=ALU.mult),
                        reads=[B_P[bank], B_orinv[p2], B_sgb[c]], writes=[B_btok[c]])

            def stage_c_new(c):
                n = gn[c]
                bank = 2 + c % 2
                p2 = c % 2
                for hh in range(H):
                    sc.op("pe", lambda e, hh=hh: e.matmul(
                        out=psum[bank][:, hh * 128:(hh + 1) * 128], lhsT=PT[:, p2, hh * 128:(hh + 1) * 128],
                        rhs=v_tok[:, c, hh * 128:(hh + 1) * 128], start=True, stop=False),
                        reads=[B_PT[p2], B_v[c]], writes=[B_P[bank]])
                    sc.op("pe", lambda e, hh=hh: e.matmul(
                        out=psum[bank][:, hh * 128:(hh + 1) * 128], lhsT=qkT[:, c, 0, hh * 128:(hh + 1) * 128],
                        rhs=Rbf[:, n % 5, hh * 128:(hh + 1) * 128], start=False, stop=True),
                        reads=[B_qkT[c], B_Rbf[n % 5]], writes=[B_P[bank]])
                osq = rB[:, 1, :]
                sc.op("act", lambda e: e.activation(out=osq, in_=psum[bank][:], func=AF.Square),
                      reads=[B_P[bank]], writes=[B_rB[1]])
                sc.op("dve", lambda e: e.tensor_reduce(
                    out=ossq[:, p2, :], in_=osq.rearrange("p (h e) -> p h e", h=H), axis=mybir.AxisListType.X,
                    op=ALU.add),
                    reads=[B_rB[1]], writes=B_ossq[p2])
                sc.op("act", lambda e: e.activation(out=ort[:, p2, :], in_=ossq[:, p2, :], func=AF.Sqrt, bias=eps_t[:],
                                                    scale=1.0 / 128),
                      reads=B_ossq[p2] + [B_eps], writes=[B_ort[p2]])
                sc.op("dve", lambda e: e.reciprocal(out=orinv[:, p2, :], in_=ort[:, p2, :]),
                      reads=[B_ort[p2]], writes=[B_orinv[p2]])
                sc.op("dve", lambda e: e.tensor_tensor(
                    out=osq.rearrange("p (h e) -> p h e", h=H), in0=psum[bank][:].rearrange("p (h e) -> p h e", h=H),
                    in1=orinv[:, p2, :].unsqueeze(2).broadcast_to([128, H, 128]), op=ALU.mult),
                    reads=[B_P[bank], B_orinv[p2]], writes=[B_rB[1]])
                sc.op("pool", lambda e: e.tensor_tensor(out=btok[:, c, :], in0=osq, in1=sgb[:, c, :], op=ALU.mult),
                      reads=[B_rB[1], B_sgb[c]], writes=[B_btok[c]])

            stage_c = stage_c_new if CMERGE else stage_c_old

            def stage_d(c, dbank=None):
                p2 = c % 2
                if dbank is None:
                    dbank = c % 2
                pbf = psum[dbank][:].bitcast(BF)
                for hh in range(H):
                    sc.op("pe", lambda e, hh=hh: e.transpose(
                        out=pbf[:, p2 * 512 + hh * 128: p2 * 512 + (hh + 1) * 128],
                        in_=btok[:, c, hh * 128:(hh + 1) * 128], identity=ident[:]),
                        reads=[B_btok[c], B_cbf], writes=[B_P[dbank]])
                sc.op("act", lambda e: e.activation(
                    out=mixT[:, 4:8, c * 128:(c + 1) * 128],
                    in_=pbf[:, p2 * 512:(p2 + 1) * 512].rearrange("p (h c) -> p h c", h=H), func=AF.Copy),
                    reads=[B_P[dbank]], writes=[B_act[4], B_act[5], B_act[6], B_act[7]])

            def pool1(g):
                bank = 7 if g % 2 == 0 else 4
                for c in range(NTT):
                    n = gn[c]
                    first_chunk = (n == 0)
                    bcur = bands[:, g * 3 + (2 if first_chunk else 0), :]
                    sc.op("pe", lambda e, c=c, bcur=bcur, fc=first_chunk, n=n: e.matmul(
                        out=psum[bank][:, c * 128:(c + 1) * 128], lhsT=up_tok[:, n % 5, g * 128:(g + 1) * 128],
                        rhs=bcur, start=True, stop=fc),
                        reads=[B_up[n % 5], B_cbf], writes=[B_P[bank]])
                    if not first_chunk:
                        sc.op("pe", lambda e, c=c, n=n: e.matmul(
                            out=psum[bank][:, c * 128:(c + 1) * 128],
                            lhsT=up_tok[:, (n - 1) % 5, g * 128:(g + 1) * 128],
                            rhs=bands[:, g * 3 + 1, :], start=False, stop=True),
                            reads=[B_up[(n - 1) % 5], B_cbf], writes=[B_P[bank]])
                sc.op("act", lambda e: e.activation(out=actT[:, 8 + g, :], in_=psum[bank][:], func=AF.Copy),
                      reads=[B_P[bank]], writes=[B_act[8 + g]])

            def pool2(g):
                bank2 = 6 + (g % 2)
                sc.op("pe", lambda e: e.matmul(
                    out=psum[bank2][:], lhsT=poolw_t[:, g, :], rhs=actT[:, 8 + g, :], start=True, stop=True),
                    reads=[B_cbf, B_act[8 + g]], writes=[B_P[bank2]])
                sc.op("act", lambda e: e.activation(
                    out=mixT[:, g, :], in_=psum[bank2][:], func=AF.Copy, scale=pscale_t[:, g:g + 1]),
                    reads=[B_P[bank2], B_const], writes=[B_act[g]])

            def wout_tt(tt):
                for dh in range(2):
                    bank = 4 + (tt % 2) * 2 + dh
                    for j in range(KC):
                        sl = wout_slots[2 * dh + j // 4]
                        wv = ring[sl].rearrange("p (k f) -> p k f", k=4)
                        sc.op("pe", lambda e, j=j, wv=wv, bank=bank: e.matmul(
                            out=psum[bank][:], lhsT=mixT[:, j, tt * 128:(tt + 1) * 128], rhs=wv[:, j % 4, :],
                            start=(j == 0), stop=(j == KC - 1)),
                            reads=[B_ring[sl], B_act[j]], writes=[B_P[bank]])
                for dh in range(2):
                    bank = 4 + (tt % 2) * 2 + dh
                    sc.op("dve", lambda e, dh=dh, bank=bank: e.tensor_tensor(
                        out=h[:, tt, dh * 512:(dh + 1) * 512], in0=psum[bank][:],
                        in1=h[:, tt, dh * 512:(dh + 1) * 512], op=ALU.add),
                        reads=[B_P[bank], B_h[hb][tt]], writes=[B_h[hb][tt]])

            def warm(n, bank):
                for _ in range(n):
                    sc.op("pe", lambda e: e.matmul(out=psum[bank][:], lhsT=ident[:],
                                                   rhs=bands[:].rearrange("p a b -> p (a b)")[:, 0:512],
                                                   start=True, stop=True),
                          reads=[B_cbf], writes=[B_P[bank]])

            proj(0)
            if tr3_pending:
                tr_tt(3, bank=7)
            proj(1)
            stage_a(0)
            proj(2)
            stage_a(1)
            if ORDER == 2:
                stage_b(0, bank=7)
                stage_b(1, bank=5)
                stage_c(0, bank=6)
            proj(3)
            for _ in range(10):
                issue_load()
            load_gain(2)
            stage_a(2)
            for g in range(4):
                pool1(g)
            if ORDER == 2:
                stage_a_tr(3)
                stage_b(2, bank=0)
                stage_c(1, bank=2)
                stage_b(3, bank=1)
                stage_c(2, bank=3)
                stage_d(0, dbank=0)
                for g in range(4):
                    pool2(g)
                stage_c(3, bank=2)
                stage_d(1, dbank=1)
                wout_tt(0)
                norm_tt(hb, 0)
                stage_d(2, dbank=0)
                wout_tt(1)
                norm_tt(hb, 1)
                stage_d(3, dbank=1)
                stage_a_kv(3, kvb=3)
                wout_tt(2)
                norm_tt(hb, 2)
                tr_tt(0)
                wout_tt(3)
                norm_tt(hb, 3)
                for _ in range(4):
                    issue_load()
                tr_tt(1)
                tr_tt(2, bank=2)
                tr_tt(3, bank=3)
                return
            stage_a(3)
            if ORDER == 0:
                stage_b(0)
                stage_b(1)
                stage_c(0)
                stage_b(2)
                stage_c(1)
                stage_d(0)
                stage_b(3)
                stage_c(2)
                stage_d(1)
                for g in range(4):
                    pool2(g)
                stage_c(3)
                stage_d(2)
                stage_d(3)
                for tt in range(NTT):
                    wout_tt(tt)
                    norm_tt(hb, tt)
                    if tt >= 1:
                        tr_tt(tt - 1)
                for _ in range(4):
                    issue_load()
                tr_tt(3)
            else:
                stage_b(0)
                stage_b(1)
                warm(WARM[0], 2)
                stage_c(0)
                stage_b(2)
                warm(WARM[1], 3)
                stage_c(1)
                stage_b(3)
                warm(WARM[2], 2)
                stage_c(2)
                stage_d(0)
                for g in range(4):
                    pool2(g)
                stage_c(3)
                stage_d(1)
                wout_tt(0)
                norm_tt(hb, 0)
                stage_d(2)
                wout_tt(1)
                norm_tt(hb, 1)
                stage_d(3)
                wout_tt(2)
                norm_tt(hb, 2)
                tr_tt(0)
                wout_tt(3)
                norm_tt(hb, 3)
                for _ in range(4):
                    issue_load()
                tr_tt(1)
                warm(WARM[3], 2)
                tr_tt(2, bank=2)
                tr_tt(3, bank=3)

        def emit_final(ti, hb, tts=range(NTT)):
            h = hbuf[hb]
            for tt in tts:
                i = 4 + tt
                sc.op("act", lambda e, tt=tt, i=i: e.activation(out=junk, in_=h[:, tt, :], func=AF.Square,
                                                              accum_out=ssq[:, i:i + 1]),
                      reads=[B_h[hb][tt]], writes=[B_ssq[i]])
                sc.op("act", lambda e, i=i: e.activation(out=rt[:, i:i + 1], in_=ssq[:, i:i + 1], func=AF.Sqrt,
                                                       bias=eps_t[:], scale=1.0 / D),
                      reads=[B_ssq[i], B_eps], writes=[B_rt[i]])
                sc.op("dve", lambda e, i=i: e.reciprocal(out=rstd[:, i:i + 1], in_=rt[:, i:i + 1]),
                      reads=[B_rt[i]], writes=[B_rstd[i]])
                sc.op("dve", lambda e, tt=tt, i=i: e.scalar_tensor_tensor(
                    out=h[:, tt, :], in0=h[:, tt, :], scalar=rstd[:, i:i + 1], in1=fgain_b[:],
                    op0=ALU.mult, op1=ALU.mult),
                    reads=[B_h[hb][tt], B_rstd[i], B_const], writes=[B_h[hb][tt]])
                store_y(ti, tt)

        load_x(0)
        load_rope(0)
        load_gain(0)
        for _ in range(NS):
            issue_load()
        for tt in range(NTT):
            norm_tt(0, tt)
        for tt in range(NTT):
            tr_tt(tt)
        for ti in range(ntiles):
            hb = ti % 2
            if ti + 1 < ntiles:
                load_x(ti + 1)
                load_rope(ti + 1)
            ffn_phase_a(ti, 0)
            load_gain(1)
            for tt in range(NTT):
                ffn_phase_b_tt(ti, hb, 0, tt)
                norm_tt(hb, tt)
                if tt >= 1:
                    tr_tt(tt - 1, bank=(4 if tt == 3 else None))
            for _ in range(NG):
                issue_load()
            mixer(ti, hb, True)
            ffn_phase_a(ti, 47)
            nxt = ti + 1 < ntiles
            if nxt:
                load_gain(0)
                for tt in range(NTT):
                    norm_tt(1 - hb, tt)
            for tt in range(NTT):
                ffn_phase_b_tt(ti, hb, 47, tt)
                if nxt and tt < 2:
                    tr_tt(2 * tt)
                    tr_tt(2 * tt + 1)
                emit_final(ti, hb, [tt])
            for _ in range(NG):
                issue_load()
        sc.wait_only("sp", out_ops)
        sc.wait_only("pool", out_ops)

        sc.finalize(engsem)
        with nc.Block() as block:
            @block.tensor
            def _(e):
                sc.emit("pe", e)

            @block.scalar
            def _(e):
                sc.emit("act", e)

            @block.vector
            def _(e):
                sc.emit("dve", e)

            @block.gpsimd
            def _(e):
                sc.emit("pool", e)

            @block.sync
            def _(e):
                sc.emit("sp", e)
    return nc


def make_consts():
    ident = np.eye(128, dtype=np.float32)
    windows = (2, 4, 8, 16)
    bands = np.zeros((128, 12, 128), np.float32)
    s = np.arange(128)[:, None]
    t = np.arange(128)[None, :]
    for g, w in enumerate(windows):
        cur = ((s <= t) & (s > t - w)).astype(np.float32) / w - (s == t).astype(np.float32)
        prev = ((s - 128 <= t) & (s - 128 > t - w)).astype(np.float32) / w
        cnt = np.minimum(t + 1, w).astype(np.float32)
        first = ((s <= t) & (s > t - w)).astype(np.float32) / cnt - (s == t).astype(np.float32)
        bands[:, g * 3 + 0, :] = cur
        bands[:, g * 3 + 1, :] = prev
        bands[:, g * 3 + 2, :] = first
    mask = (t >= s).astype(np.float32)
    hh = np.arange(H, dtype=np.float64)
    log_gamma = np.log1p(-np.exp2(-5.0 - hh))
    pos = np.arange(C, dtype=np.float64)
    qfac = np.exp(log_gamma[None, :] * (pos[:, None] + 1.0))
    kfac = np.exp(-log_gamma[None, :] * (pos[:, None] + 1.0)) * (128 ** -0.5)
    fac = np.concatenate([qfac, kfac], axis=1).astype(np.float32)
    gc = np.repeat(np.exp(log_gamma * C)[None, :, None], 128, axis=0)
    gc = np.repeat(gc, 128, axis=2).reshape(128, 512).astype(np.float32)
    inv_freq = 1.0 / (10000.0 ** (np.arange(0, 128, 2, dtype=np.float32) / np.float32(128)))
    ang = np.arange(S, dtype=np.float32)[:, None] * inv_freq[None, :].astype(np.float32)
    rope = np.concatenate([np.cos(ang), np.sin(ang)], axis=1).astype(np.float32)
    return dict(c_ident=ident, c_bands=bands, c_mask=mask, c_fac=fac, c_gc=gc, c_rope=rope)


_NC_CACHE = {}


def kernel(x, ffn1_norm, ffn1_w1, ffn1_w3, ffn1_w2, mix_norm, w_in, pool_w, pool_scale,
           ret_norm, w_out, ffn2_norm, ffn2_w1, ffn2_w3, ffn2_w2, final_norm, _ntiles=S // T):
    f = lambda a: np.ascontiguousarray(np.asarray(a, dtype=np.float32))
    x = f(x)
    B = x.shape[0]
    ntok = _ntiles * T
    shared = dict(
        w1a=f(ffn1_w1)[0], w3a=f(ffn1_w3)[0], w2a=f(ffn1_w2)[0],
        w1b=f(ffn2_w1)[0], w3b=f(ffn2_w3)[0], w2b=f(ffn2_w2)[0],
        w_in=f(w_in)[0], w_out=f(w_out)[0], pool_w=f(pool_w)[0],
        gains=np.ascontiguousarray(np.stack([f(ffn1_norm)[0], f(mix_norm)[0], f(ffn2_norm)[0]])),
        fgain=f(final_norm), retg=f(ret_norm)[0],
        pscale=np.ascontiguousarray(f(pool_scale)[0].reshape(4, 128).T),
    )
    shared.update(make_consts())
    if _ntiles not in _NC_CACHE:
        _NC_CACHE[_ntiles] = build_nc(_ntiles)
    nc = _NC_CACHE[_ntiles]
    in_maps = []
    for b in range(B):
        m = dict(shared)
        m["x"] = np.ascontiguousarray(x[b, :ntok])
        in_maps.append(m)
    res = run_bass_kernel_spmd(nc, in_maps, core_ids=list(range(B)))
    return np.stack([np.asarray(r["y"], dtype=np.float32) for r in res.results], axis=0)
```

```python
import numpy as np
from contextlib import ExitStack
import concourse.bass as bass
import concourse.mybir as mybir
from concourse.bass_utils import run_bass_kernel_spmd

F32 = mybir.dt.float32
BF = mybir.dt.bfloat16
AF = mybir.ActivationFunctionType
ALU = mybir.AluOpType

D = 1024
S = 4096
DFF = 2816
T = 512
NTT = 4
KC = 8
NF = 22
NG = 11
H = 4
C = 128
EPS = 1e-6
NSTREAM = 80
NS = 16
ENGS = ("pe", "act", "dve", "pool", "sp")
WARM = [8, 8, 0, 8]
CMERGE = 0
ORDER = 1


class Buf:
    __slots__ = ("w", "r", "rd", "name", "const")

    def __init__(self, name="", const=False):
        self.w = None
        self.r = {}
        self.rd = []
        self.name = name
        self.const = const


class Op:
    __slots__ = ("eng", "fn", "deps", "needed", "val", "sem", "ndma", "seq")

    def __init__(self, eng, fn):
        self.eng = eng
        self.fn = fn
        self.deps = []
        self.needed = False
        self.val = None
        self.sem = None
        self.ndma = 0
        self.seq = 0


class Sched:
    def __init__(self):
        self.ops = {e: [] for e in ENGS}
        self.dma_cum = {}

    def _track(self, o, reads, writes):
        o.seq = len(self.ops[o.eng])
        cand = []
        for b in reads:
            if b.w is not None:
                cand.append(b.w)
        for b in writes:
            if b.w is not None:
                cand.append(b.w)
            cand.extend(b.r.values())
            cand.extend(b.rd)
        best = {}
        for d in cand:
            if d is o:
                continue
            if d.ndma:
                key = ("dma", d.sem.num)
                if key not in best or best[key].val < d.val:
                    best[key] = d
            else:
                if d.eng == "pe" and o.eng == "pe":
                    continue
                key = ("eng", d.eng)
                if key not in best or best[key].seq < d.seq:
                    best[key] = d
        for d in best.values():
            o.deps.append(d)
            d.needed = True
        for b in reads:
            if b.const:
                continue
            if o.ndma:
                b.rd.append(o)
            else:
                b.r[o.eng] = o
        for b in writes:
            b.w = o
            b.r = {}
            b.rd = []

    def op(self, eng, fn, reads=(), writes=()):
        o = Op(eng, fn)
        self._track(o, reads, writes)
        self.ops[eng].append(o)
        return o

    def dma(self, eng, fn, sem, n, reads=(), writes=()):
        o = Op(eng, fn)
        o.ndma = n
        o.sem = sem
        key = sem.num
        self.dma_cum[key] = self.dma_cum.get(key, 0) + 16 * n
        o.val = self.dma_cum[key]
        self._track(o, reads, writes)
        self.ops[eng].append(o)
        return o

    def wait_only(self, eng, deps):
        o = Op(eng, None)
        for d in deps:
            o.deps.append(d)
            d.needed = True
        self.ops[eng].append(o)
        return o

    def finalize(self, engsem):
        for e in ENGS:
            cnt = 0
            for o in self.ops[e]:
                if o.ndma:
                    continue
                if o.needed and o.fn is not None:
                    cnt += 1
                    o.val = cnt
                    o.sem = engsem[e]

    def emit(self, eng, h):
        waited = {}
        for o in self.ops[eng]:
            need = {}
            for d in o.deps:
                k = d.sem.num
                if waited.get(k, 0) >= d.val:
                    continue
                if k not in need or need[k][1] < d.val:
                    need[k] = (d.sem, d.val)
            for k, (sem, val) in need.items():
                h.wait_ge(sem, val)
                waited[k] = val
            if o.fn is None:
                continue
            r = o.fn(h)
            if o.ndma:
                assert len(r) == o.ndma
                for ins in r:
                    ins.then_inc(o.sem, 16)
            elif o.needed:
                r.then_inc(o.sem, 1)


def build_nc(ntiles=S // T):
    nc = bass.Bass("TRN2", target_bir_lowering=False)
    ntok = ntiles * T

    def din(name, shape, dt=F32):
        return nc.dram_tensor(name, list(shape), dt, kind="ExternalInput").ap()

    x = din("x", [ntok, D])
    y = nc.dram_tensor("y", [ntok, D], F32, kind="ExternalOutput").ap()
    w1 = [din("w1a", [D, DFF]), din("w1b", [D, DFF])]
    w3 = [din("w3a", [D, DFF]), din("w3b", [D, DFF])]
    w2 = [din("w2a", [DFF, D]), din("w2b", [DFF, D])]
    w_in = din("w_in", [D, 2560])
    w_out = din("w_out", [D, D])
    pool_w = din("pool_w", [4, 128, 128])
    gains = din("gains", [3, D])
    fgain = din("fgain", [D])
    retg = din("retg", [512])
    pscale = din("pscale", [128, 4])
    c_ident = din("c_ident", [128, 128])
    c_bands = din("c_bands", [128, 12, 128])
    c_mask = din("c_mask", [128, 128])
    c_fac = din("c_fac", [128, 8])
    c_gc = din("c_gc", [128, 512])
    c_rope = din("c_rope", [S, 128])
    scr = nc.dram_tensor("scr", [NSTREAM, 128, 2048], BF, kind="Internal").ap()

    es = ExitStack()
    with es:
        def sb(name, shape, dt):
            return es.enter_context(nc.sbuf_tensor(name, list(shape), dt))

        def sem(name):
            return es.enter_context(nc.semaphore(name))

        hbuf = [sb(f"h{i}", [128, NTT, D], F32) for i in range(2)]
        xnb = sb("xnb", [128, NTT, D], BF)
        xnT = sb("xnT", [128, KC, T], BF)
        actT = sb("actT", [128, NF, T], BF)
        ring_all = sb("ring_all", [128, NS, 2048], BF)
        ring = [ring_all[:, i, :] for i in range(NS)]
        silu_t = [sb(f"silu{i}", [128, T], F32) for i in range(2)]
        gain_cur = sb("gain_cur", [128, D], F32)
        fgain_b = sb("fgain_b", [128, D], F32)
        retg_b = sb("retg_b", [128, 512], F32)
        pscale_t = sb("pscale_t", [128, 4], F32)
        rope_t = [sb(f"rope{i}", [128, NTT, 128], F32) for i in range(2)]
        ident = sb("ident", [128, 128], BF)
        bands = sb("bands", [128, 12, 128], BF)
        mask_t = sb("mask_t", [128, 128], F32)
        fac_t = sb("fac_t", [128, 8], F32)
        gc_t = sb("gc_t", [128, 512], F32)
        eps_t = sb("eps_t", [128, 1], F32)
        poolw_t = sb("poolw_t", [128, 4, 128], BF)
        ssq = sb("ssq", [128, 8], F32)
        rt = sb("rt", [128, 8], F32)
        rstd = sb("rstd", [128, 8], F32)
        up_tok = sb("up_tok", [128, 5, 512], BF)
        v_tok = sb("v_tok", [128, 4, 512], BF)
        sg32 = sb("sg32", [128, 512], F32)
        sgb = sb("sgb", [128, 4, 512], BF)
        rB = sb("rB", [128, 2, 512], F32)
        qrot = sb("qrot", [128, 2, 512], BF)
        krot = sb("krot", [128, 2, 512], BF)
        qkT = sb("qkT", [128, 4, 2, 512], BF)
        PT = sb("PT", [128, 2, 512], BF)
        R32 = sb("R32", [128, 512], F32)
        Rbf = sb("Rbf", [128, 5, 512], BF)
        btok = sb("btok", [128, 4, 512], BF)
        ossq = sb("ossq", [128, 2, 4], F32)
        ort = sb("ort", [128, 2, 4], F32)
        orinv = sb("orinv", [128, 2, 4], F32)
        junk = rB[:].rearrange("p a b -> p (a b)").bitcast(BF)[:, 0:D]
        mixT = actT
        f32v = actT[:, 12:22, :].rearrange("p a b -> p (a b)").bitcast(F32)
        qs32 = [f32v[:, 0:512], f32v[:, 512:1024]]
        ks32 = [f32v[:, 1024:1536], f32v[:, 1536:2048]]
        tmpKV = f32v[:, 2048:2560]

        psum = [es.enter_context(nc.psum_tensor(f"ps{i}", [128, 512], F32)) for i in range(8)]

        engsem = {e: sem(f"sem_{e}") for e in ("pe", "act", "dve", "pool")}
        ring_sem = [sem(f"ring{i}") for i in range(NS)]
        ring_sw_sem = [sem(f"ringsw{i}") for i in range(NS)]
        fill_sem = [sem(f"fill{i}") for i in range(NSTREAM // 2)]
        x_sem = [sem("x0"), sem("x1")]
        o_sem = [sem("o0"), sem("o1")]
        rope_sem = [sem("rp0"), sem("rp1")]
        c_sem = sem("consts")
        pw_sem = sem("poolw")
        g_sem = sem("gain")

        sc = Sched()

        B_h = [[Buf(f"h{b}_{t}") for t in range(NTT)] for b in range(2)]
        B_xnb = [Buf() for _ in range(NTT)]
        B_xnT = [Buf() for _ in range(NTT)]
        B_act = [Buf(f"act{f}") for f in range(NF)]
        B_ring = [Buf(f"ring{i}") for i in range(NS)]
        B_scr = [Buf() for _ in range(NSTREAM)]
        B_P = [Buf(f"P{i}") for i in range(8)]
        B_silu = [Buf(), Buf()]
        B_const = Buf("const", const=True)
        B_cbf = Buf("cbf", const=True)
        B_gain = Buf("gain")
        B_rope = [Buf(), Buf()]
        B_ssq = [Buf() for _ in range(8)]
        B_rt = [Buf() for _ in range(8)]
        B_rstd = [Buf() for _ in range(8)]
        B_up = [Buf() for _ in range(5)]
        B_v = [Buf() for _ in range(4)]
        B_sg32 = Buf()
        B_sgb = [Buf() for _ in range(4)]
        B_rB = [Buf(), Buf()]
        B_qrot = [Buf(), Buf()]
        B_krot = [Buf(), Buf()]
        B_qkT = [Buf() for _ in range(4)]
        B_PT = [Buf(), Buf()]
        B_R32 = Buf()
        B_Rbf = [Buf() for _ in range(5)]
        B_btok = [Buf() for _ in range(4)]
        B_ossq = [[Buf() for _ in range(4)] for _ in range(2)]
        B_ort = [Buf(), Buf()]
        B_orinv = [Buf(), Buf()]
        Bq32 = [B_act[12 + 0], B_act[12 + 1]], [B_act[12 + 2], B_act[12 + 3]]
        Bk32 = [B_act[12 + 4], B_act[12 + 5]], [B_act[12 + 6], B_act[12 + 7]]
        BtKV = [B_act[20], B_act[21]]

        def stream_src(k):
            def blk(w, half, G):
                return w[half * 512:(half + 1) * 512, G * 512:(G + 1) * 512].rearrange("(k p) f -> p k f", p=128)

            def colslot(w, c0):
                return w[:, c0:c0 + 256].rearrange("(k p) f -> p k f", p=128)

            def rowslot(w, j):
                return w[j * 256:(j + 1) * 256, :].rearrange("(a p) d -> p a d", p=128)

            if k < 33:
                ff, kk = 0, k
            elif k < 47:
                kk = k - 33
                if kk < 10:
                    return blk(w_in, kk % 2, kk // 2)
                kk -= 10
                return blk(w_out, kk % 2, kk // 2)
            else:
                ff, kk = 1, k - 47
            if kk < 20:
                G, r = kk // 4, kk % 4
                return blk(w1[ff] if r < 2 else w3[ff], r % 2, G)
            if kk < 22:
                return colslot(w1[ff] if kk == 20 else w3[ff], 2560)
            return rowslot(w2[ff], kk - 22)

        def const_loads(e):
            r = []
            r.append(e.dma_start(out=mask_t[:], in_=c_mask))
            r.append(e.dma_start(out=fac_t[:], in_=c_fac))
            r.append(e.dma_start(out=gc_t[:], in_=c_gc))
            r.append(e.dma_start(out=pscale_t[:], in_=pscale))
            r.append(e.dma_start(out=fgain_b[:], in_=fgain.partition_broadcast(128)))
            r.append(e.dma_start(out=retg_b[:], in_=retg.partition_broadcast(128)))
            return r

        sc.dma("sp", const_loads, c_sem, 6, writes=[B_const])

        def cast_consts(e):
            return [e.dma_start(out=ident[:], in_=c_ident),
                    e.dma_start(out=bands[:], in_=c_bands),
                    e.dma_start(out=poolw_t[:], in_=pool_w.rearrange("g c e -> c g e"))]

        sc.dma("pool", cast_consts, pw_sem, 3, writes=[B_cbf])
        B_eps = Buf("eps", const=True)
        sc.op("dve", lambda e: e.memset(eps_t[:], EPS), writes=[B_eps])
        DIRECT0 = 1
        if not DIRECT0:
            for k in range(NSTREAM):
                def fill(e, k=k):
                    src = stream_src(k)
                    dst = scr[k].rearrange("p (a b) -> p a b", a=src.shape[1])
                    return [e.dma_start(out=dst, in_=src)]
                sc.dma("pool", fill, fill_sem[k // 2], 1, writes=[B_scr[k]])
            for k in range(0, NSTREAM, 2):
                B_scr[k].w = B_scr[k + 1].w
        sc.op("pool", lambda e: e.memset(R32[:], 0.0), writes=[B_R32])
        sc.op("pool", lambda e: e.memset(Rbf[:, 0, :], 0.0), writes=[B_Rbf[0]])

        state = {"next_load": 0, "total": ntiles * NSTREAM}

        def issue_load():
            n = state["next_load"]
            if n >= state["total"]:
                return
            state["next_load"] = n + 1
            s = n % NS
            k = n % NSTREAM
            SPREAD = 4
            tq = n // NSTREAM
            kq = k // (NSTREAM // SPREAD)
            if DIRECT0 and kq >= tq:
                src = stream_src(k)
                dst = ring[s].rearrange("p (a b) -> p a b", a=src.shape[1])
                sc.dma("pool", lambda e: [e.dma_start(out=dst, in_=src)], ring_sw_sem[s], 1, writes=[B_ring[s]])
                if kq == tq:
                    so = sc.dma("sp", lambda e: [e.dma_start(out=scr[k], in_=ring[s])], fill_sem[k // 2], 1,
                                reads=[B_ring[s]], writes=[B_scr[k]])
                    if k % 2 == 0:
                        state["even_store"] = so
                    else:
                        state["even_store"].val = so.val
                return
            sc.dma("sp", lambda e: [e.dma_start(out=ring[s], in_=scr[k])], ring_sem[s], 1,
                   reads=[B_scr[k]], writes=[B_ring[s]])

        def slot_of(ti, k):
            return (ti * NSTREAM + k) % NS

        def load_x(ti):
            b = ti % 2
            src = x[ti * T:(ti + 1) * T, :].rearrange("(t p) d -> p t d", p=128)
            sc.dma("sp", lambda e: [e.dma_start(out=hbuf[b][:], in_=src)], x_sem[b], 1, writes=B_h[b])

        def load_rope(ti):
            b = ti % 2
            src = c_rope[ti * T:(ti + 1) * T, :].rearrange("(t p) d -> p t d", p=128)
            sc.dma("sp", lambda e: [e.dma_start(out=rope_t[b][:], in_=src)], rope_sem[b], 1, writes=[B_rope[b]])

        def load_gain(gi):
            sc.dma("sp", lambda e: [e.dma_start(out=gain_cur[:], in_=gains[gi].partition_broadcast(128))],
                   g_sem, 1, writes=[B_gain])

        out_ops = []

        def store_y(ti, tt):
            b = ti % 2
            dst = y[ti * T + tt * 128: ti * T + (tt + 1) * 128, :]
            out_ops.append(sc.dma("pool", lambda e: [e.dma_start(out=dst, in_=hbuf[b][:, tt, :])], o_sem[b], 1,
                                  reads=[B_h[b][tt]]))

        def norm_tt(hb, tt):
            h = hbuf[hb]
            sc.op("act", lambda e: e.activation(out=xnb[:, tt, :], in_=h[:, tt, :], func=AF.Square,
                                                accum_out=ssq[:, tt:tt + 1]),
                  reads=[B_h[hb][tt]], writes=[B_xnb[tt], B_ssq[tt]])
            sc.op("act", lambda e: e.activation(out=rt[:, tt:tt + 1], in_=ssq[:, tt:tt + 1], func=AF.Sqrt,
                                                bias=eps_t[:], scale=1.0 / D),
                  reads=[B_ssq[tt], B_eps], writes=[B_rt[tt]])
            sc.op("dve", lambda e: e.reciprocal(out=rstd[:, tt:tt + 1], in_=rt[:, tt:tt + 1]),
                  reads=[B_rt[tt]], writes=[B_rstd[tt]])
            sc.op("dve", lambda e: e.scalar_tensor_tensor(
                out=xnb[:, tt, :], in0=h[:, tt, :], scalar=rstd[:, tt:tt + 1], in1=gain_cur[:],
                op0=ALU.mult, op1=ALU.mult),
                reads=[B_h[hb][tt], B_rstd[tt], B_gain], writes=[B_xnb[tt]])

        def tr_tt(tt, bank=None):
            if bank is None:
                bank = tt % 2
            pbf = psum[bank][:].bitcast(BF)
            for kc in range(KC):
                sc.op("pe", lambda e, kc=kc: e.transpose(
                    out=pbf[:, kc * 128:(kc + 1) * 128], in_=xnb[:, tt, kc * 128:(kc + 1) * 128], identity=ident[:]),
                    reads=[B_xnb[tt], B_cbf], writes=[B_P[bank]])
            src = pbf.rearrange("p (k t) -> p k t", k=KC)
            dst = xnT[:, :, tt * 128:(tt + 1) * 128]
            if tt % 2 == 0:
                sc.op("act", lambda e: e.activation(out=dst, in_=src, func=AF.Copy),
                      reads=[B_P[bank]], writes=[B_xnT[tt]])
            else:
                sc.op("dve", lambda e: e.tensor_copy(out=dst, in_=src),
                      reads=[B_P[bank]], writes=[B_xnT[tt]])

        def ffn_phase_a(ti, kbase):
            def fchunk(f, j, w1s, w3s, nk):
                pa, pb = (0, 1) if f % 2 == 0 else (2, 3)
                for (ws, pbk) in ((w1s, pa), (w3s, pb)):
                    for kc in range(KC):
                        sl, wv = ws[kc // nk]
                        sc.op("pe", lambda e, wv=wv, kc=kc, pbk=pbk: e.matmul(
                            out=psum[pbk][:], lhsT=wv[:, kc % nk, j * 128:(j + 1) * 128], rhs=xnT[:, kc, :],
                            start=(kc == 0), stop=(kc == KC - 1)),
                            reads=[B_ring[sl]] + B_xnT, writes=[B_P[pbk]])
                st = f % 2
                sc.op("act", lambda e: e.activation(out=silu_t[st][:], in_=psum[pa][:], func=AF.Silu),
                      reads=[B_P[pa]], writes=[B_silu[st]])
                sc.op("dve", lambda e: e.tensor_tensor(
                    out=actT[:, f, :], in0=silu_t[st][:], in1=psum[pb][:], op=ALU.mult),
                    reads=[B_silu[st], B_P[pb]], writes=[B_act[f]])

            for G in range(5):
                sl = [slot_of(ti, kbase + 4 * G + i) for i in range(4)]
                vw = [(x, ring[x].rearrange("p (k f) -> p k f", k=4)) for x in sl]
                for j in range(4):
                    fchunk(4 * G + j, j, vw[0:2], vw[2:4], 4)
                for _ in range(4):
                    issue_load()
            s1 = slot_of(ti, kbase + 20)
            s3 = slot_of(ti, kbase + 21)
            v1 = [(s1, ring[s1].rearrange("p (k f) -> p k f", k=KC))]
            v3 = [(s3, ring[s3].rearrange("p (k f) -> p k f", k=KC))]
            for j in range(2):
                fchunk(20 + j, j, v1, v3, 8)
            issue_load()
            issue_load()

        def ffn_phase_b_tt(ti, hb, kbase, tt):
            h = hbuf[hb]
            for dh in range(2):
                bank = 4 + (tt % 2) * 2 + dh
                for f in range(NF):
                    sl = slot_of(ti, kbase + 22 + f // 2)
                    wv = ring[sl].rearrange("p (a d) -> p a d", a=2)
                    sc.op("pe", lambda e, wv=wv, f=f, dh=dh, bank=bank: e.matmul(
                        out=psum[bank][:], lhsT=actT[:, f, tt * 128:(tt + 1) * 128],
                        rhs=wv[:, f % 2, dh * 512:(dh + 1) * 512], start=(f == 0), stop=(f == NF - 1)),
                        reads=[B_ring[sl], B_act[f]], writes=[B_P[bank]])
                sc.op("dve", lambda e, dh=dh, bank=bank: e.scalar_tensor_tensor(
                    out=h[:, tt, dh * 512:(dh + 1) * 512], in0=psum[bank][:], scalar=0.5,
                    in1=h[:, tt, dh * 512:(dh + 1) * 512], op0=ALU.mult, op1=ALU.add),
                    reads=[B_P[bank], B_h[hb][tt]], writes=[B_h[hb][tt]])

        def mixer(ti, hb, tr3_pending):
            h = hbuf[hb]
            rp = rope_t[ti % 2]
            B_rp = B_rope[ti % 2]
            kbase = 33
            win_slots = [slot_of(ti, kbase + i) for i in range(10)]
            wout_slots = [slot_of(ti, kbase + 10 + i) for i in range(4)]
            gn = [ti * 4 + c for c in range(4)]

            def rope(eng, X32, Bsrc, ri, dst, Bdst, c):
                X = X32.rearrange("p (h t d) -> p h t d", h=H, t=2)
                Bv = rB[:, ri, :].rearrange("p (h t d) -> p h t d", h=H, t=2)
                O = dst.rearrange("p (h t d) -> p h t d", h=H, t=2)
                cos4 = rp[:, c, 0:64].unsqueeze(1).unsqueeze(1).broadcast_to([128, H, 2, 64])
                sin3 = rp[:, c, 64:128].unsqueeze(1).broadcast_to([128, H, 64])
                sc.op(eng, lambda e: e.tensor_tensor(out=Bv[:, :, 0, :], in0=X[:, :, 1, :], in1=sin3, op=ALU.mult),
                      reads=Bsrc + [B_rp], writes=[B_rB[ri]])
                sc.op(eng, lambda e: e.tensor_tensor(out=Bv[:, :, 1, :], in0=X[:, :, 0, :], in1=sin3, op=ALU.mult),
                      reads=Bsrc + [B_rp], writes=[B_rB[ri]])
                sc.op(eng, lambda e: e.tensor_tensor(out=X, in0=X, in1=cos4, op=ALU.mult),
                      reads=Bsrc + [B_rp], writes=Bsrc)
                sc.op(eng, lambda e: e.tensor_tensor(out=O[:, :, 0, :], in0=X[:, :, 0, :], in1=Bv[:, :, 0, :],
                                                     op=ALU.subtract),
                      reads=Bsrc + [B_rB[ri]], writes=[Bdst])
                sc.op(eng, lambda e: e.tensor_tensor(out=O[:, :, 1, :], in0=X[:, :, 1, :], in1=Bv[:, :, 1, :],
                                                     op=ALU.add),
                      reads=Bsrc + [B_rB[ri]], writes=[Bdst])

            def proj(c):
                n = gn[c]
                p2 = c % 2
                for bi in range(5):
                    for kc in range(KC):
                        sl = win_slots[bi * 2 + kc // 4]
                        wv = ring[sl].rearrange("p (k f) -> p k f", k=4)
                        sc.op("pe", lambda e, kc=kc, bi=bi, wv=wv: e.matmul(
                            out=psum[bi][:], lhsT=xnT[:, kc, c * 128:(c + 1) * 128], rhs=wv[:, kc % 4, :],
                            start=(kc == 0), stop=(kc == KC - 1)),
                            reads=[B_ring[sl], B_xnT[c]], writes=[B_P[bi]])
                    if bi == 0:
                        sc.op("act", lambda e: e.activation(out=up_tok[:, n % 5, :], in_=psum[0][:], func=AF.Copy),
                              reads=[B_P[0]], writes=[B_up[n % 5]])
                    elif bi == 1:
                        for hh in range(H):
                            sc.op("act", lambda e, hh=hh: e.activation(
                                out=qs32[p2][:, hh * 128:(hh + 1) * 128], in_=psum[1][:, hh * 128:(hh + 1) * 128],
                                func=AF.Copy, scale=fac_t[:, hh:hh + 1]),
                                reads=[B_P[1], B_const], writes=[Bq32[p2][hh // 2]])
                        rope("pool", qs32[p2], Bq32[p2], 0, qrot[:, p2, :], B_qrot[p2], c)
                    elif bi == 2:
                        for hh in range(H):
                            sc.op("act", lambda e, hh=hh: e.activation(
                                out=ks32[p2][:, hh * 128:(hh + 1) * 128], in_=psum[2][:, hh * 128:(hh + 1) * 128],
                                func=AF.Copy, scale=fac_t[:, 4 + hh:5 + hh]),
                                reads=[B_P[2], B_const], writes=[Bk32[p2][hh // 2]])
                        rope("dve", ks32[p2], Bk32[p2], 1, krot[:, p2, :], B_krot[p2], c)
                    elif bi == 3:
                        sc.op("act", lambda e: e.activation(out=v_tok[:, c, :], in_=psum[3][:], func=AF.Copy),
                              reads=[B_P[3]], writes=[B_v[c]])
                    else:
                        sc.op("act", lambda e: e.activation(out=sg32[:], in_=psum[4][:], func=AF.Silu),
                              reads=[B_P[4]], writes=[B_sg32])
                        sc.op("pool", lambda e: e.tensor_tensor(out=sgb[:, c, :], in0=sg32[:], in1=retg_b[:],
                                                                op=ALU.mult),
                              reads=[B_sg32, B_const], writes=[B_sgb[c]])

            def stage_a_tr(c):
                n = gn[c]
                p2 = c % 2
                pbf = psum[5][:].bitcast(BF)
                for qi, (srcT, Bs) in enumerate(((qrot, B_qrot[p2]), (krot, B_krot[p2]))):
                    for hh in range(H):
                        sc.op("pe", lambda e, qi=qi, hh=hh, srcT=srcT: e.transpose(
                            out=pbf[:, qi * 512 + hh * 128: qi * 512 + (hh + 1) * 128],
                            in_=srcT[:, p2, hh * 128:(hh + 1) * 128], identity=ident[:]),
                            reads=[Bs, B_cbf], writes=[B_P[5]])
                sc.op("dve", lambda e: e.tensor_copy(out=qkT[:, c, :, :].rearrange("p a b -> p (a b)"), in_=pbf),
                      reads=[B_P[5]], writes=[B_qkT[c]])

            def stage_a_kv(c, kvb=6):
                n = gn[c]
                p2 = c % 2
                for hh in range(H):
                    sc.op("pe", lambda e, hh=hh: e.matmul(
                        out=psum[kvb][:, hh * 128:(hh + 1) * 128], lhsT=krot[:, p2, hh * 128:(hh + 1) * 128],
                        rhs=v_tok[:, c, hh * 128:(hh + 1) * 128], start=True, stop=True),
                        reads=[B_krot[p2], B_v[c]], writes=[B_P[kvb]])
                sc.op("dve", lambda e: e.tensor_tensor(out=tmpKV, in0=psum[kvb][:], in1=R32[:], op=ALU.add),
                      reads=[B_P[kvb], B_R32], writes=BtKV)
                sc.op("dve", lambda e: e.tensor_tensor(out=R32[:], in0=tmpKV, in1=gc_t[:], op=ALU.mult),
                      reads=BtKV + [B_const], writes=[B_R32])
                sc.op("dve", lambda e: e.tensor_copy(out=Rbf[:, (n + 1) % 5, :], in_=R32[:]),
                      reads=[B_R32], writes=[B_Rbf[(n + 1) % 5]])


            def stage_a(c):
                stage_a_tr(c)
                stage_a_kv(c)

            def stage_b(c, bank=None):
                if bank is None:
                    bank = c % 2
                p2 = c % 2
                for hh in range(H):
                    sc.op("pe", lambda e, hh=hh: e.matmul(
                        out=psum[bank][:, hh * 128:(hh + 1) * 128], lhsT=qkT[:, c, 1, hh * 128:(hh + 1) * 128],
                        rhs=qkT[:, c, 0, hh * 128:(hh + 1) * 128], start=True, stop=True),
                        reads=[B_qkT[c]], writes=[B_P[bank]])
                m4 = mask_t[:].unsqueeze(1).broadcast_to([128, H, 128])
                sc.op("dve", lambda e: e.tensor_tensor(
                    out=PT[:, p2, :].rearrange("p (h c) -> p h c", h=H),
                    in0=psum[bank][:].rearrange("p (h c) -> p h c", h=H), in1=m4, op=ALU.mult),
                    reads=[B_P[bank], B_const], writes=[B_PT[p2]])

            def stage_c_old(c, bank=None):
                n = gn[c]
                if bank is None:
                    bank = 2 + c % 2
                p2 = c % 2
                for hh in range(H):
                    sc.op("pe", lambda e, hh=hh: e.matmul(
                        out=psum[bank][:, hh * 128:(hh + 1) * 128], lhsT=PT[:, p2, hh * 128:(hh + 1) * 128],
                        rhs=v_tok[:, c, hh * 128:(hh + 1) * 128], start=True, stop=False),
                        reads=[B_PT[p2], B_v[c]], writes=[B_P[bank]])
                    sc.op("pe", lambda e, hh=hh: e.matmul(
                        out=psum[bank][:, hh * 128:(hh + 1) * 128], lhsT=qkT[:, c, 0, hh * 128:(hh + 1) * 128],
                        rhs=Rbf[:, n % 5, hh * 128:(hh + 1) * 128], start=False, stop=True),
                        reads=[B_qkT[c], B_Rbf[n % 5]], writes=[B_P[bank]])
                for hh in range(H):
                    sc.op("act", lambda e, hh=hh: e.activation(
                        out=junk[:, hh * 128:(hh + 1) * 128], in_=psum[bank][:, hh * 128:(hh + 1) * 128],
                        func=AF.Square, accum_out=ossq[:, p2, hh:hh + 1]),
                        reads=[B_P[bank]], writes=[B_ossq[p2][hh]])
                sc.op("act", lambda e: e.activation(out=ort[:, p2, :], in_=ossq[:, p2, :], func=AF.Sqrt, bias=eps_t[:],
                                                    scale=1.0 / 128),
                      reads=B_ossq[p2] + [B_eps], writes=[B_ort[p2]])
                sc.op("dve", lambda e: e.reciprocal(out=orinv[:, p2, :], in_=ort[:, p2, :]),
                      reads=[B_ort[p2]], writes=[B_orinv[p2]])
                for hh in range(H):
                    sc.op("dve", lambda e, hh=hh: e.scalar_tensor_tensor(
                        out=btok[:, c, hh * 128:(hh + 1) * 128], in0=psum[bank][:, hh * 128:(hh + 1) * 128],
                        scalar=orinv[:, p2, hh:hh + 1], in1=sgb[:, c, hh * 128:(hh + 1) * 128],
                        op0=ALU.mult, op1=ALU.mult),
                        reads=[B_P[bank], B_orinv[p2], B_sgb[c]], writes=[B_btok[c]])

            def stage_c_new(c):
                n = gn[c]
                bank = 2 + c % 2
                p2 = c % 2
                for hh in range(H):
                    sc.op("pe", lambda e, hh=hh: e.matmul(
                        out=psum[bank][:, hh * 128:(hh + 1) * 128], lhsT=PT[:, p2, hh * 128:(hh + 1) * 128],
                        rhs=v_tok[:, c, hh * 128:(hh + 1) * 128], start=True, stop=False),
                        reads=[B_PT[p2], B_v[c]], writes=[B_P[bank]])
                    sc.op("pe", lambda e, hh=hh: e.matmul(
                        out=psum[bank][:, hh * 128:(hh + 1) * 128], lhsT=qkT[:, c, 0, hh * 128:(hh + 1) * 128],
                        rhs=Rbf[:, n % 5, hh * 128:(hh + 1) * 128], start=False, stop=True),
                        reads=[B_qkT[c], B_Rbf[n % 5]], writes=[B_P[bank]])
                osq = rB[:, 1, :]
                sc.op("act", lambda e: e.activation(out=osq, in_=psum[bank][:], func=AF.Square),
                      reads=[B_P[bank]], writes=[B_rB[1]])
                sc.op("dve", lambda e: e.tensor_reduce(
                    out=ossq[:, p2, :], in_=osq.rearrange("p (h e) -> p h e", h=H), axis=mybir.AxisListType.X,
                    op=ALU.add),
                    reads=[B_rB[1]], writes=B_ossq[p2])
                sc.op("act", lambda e: e.activation(out=ort[:, p2, :], in_=ossq[:, p2, :], func=AF.Sqrt, bias=eps_t[:],
                                                    scale=1.0 / 128),
                      reads=B_ossq[p2] + [B_eps], writes=[B_ort[p2]])
                sc.op("dve", lambda e: e.reciprocal(out=orinv[:, p2, :], in_=ort[:, p2, :]),
                      reads=[B_ort[p2]], writes=[B_orinv[p2]])
                sc.op("dve", lambda e: e.tensor_tensor(
                    out=osq.rearrange("p (h e) -> p h e", h=H), in0=psum[bank][:].rearrange("p (h e) -> p h e", h=H),
                    in1=orinv[:, p2, :].unsqueeze(2).broadcast_to([128, H, 128]), op=ALU.mult),
                    reads=[B_P[bank], B_orinv[p2]], writes=[B_rB[1]])
                sc.op("pool", lambda e: e.tensor_tensor(out=btok[:, c, :], in0=osq, in1=sgb[:, c, :], op=ALU.mult),
                      reads=[B_rB[1], B_sgb[c]], writes=[B_btok[c]])

            stage_c = stage_c_new if CMERGE else stage_c_old

            def stage_d(c, dbank=None):
                p2 = c % 2
                if dbank is None:
                    dbank = c % 2
                pbf = psum[dbank][:].bitcast(BF)
                for hh in range(H):
                    sc.op("pe", lambda e, hh=hh: e.transpose(
                        out=pbf[:, p2 * 512 + hh * 128: p2 * 512 + (hh + 1) * 128],
                        in_=btok[:, c, hh * 128:(hh + 1) * 128], identity=ident[:]),
                        reads=[B_btok[c], B_cbf], writes=[B_P[dbank]])
                sc.op("act", lambda e: e.activation(
                    out=mixT[:, 4:8, c * 128:(c + 1) * 128],
                    in_=pbf[:, p2 * 512:(p2 + 1) * 512].rearrange("p (h c) -> p h c", h=H), func=AF.Copy),
                    reads=[B_P[dbank]], writes=[B_act[4], B_act[5], B_act[6], B_act[7]])

            def pool1(g):
                bank = 7 if g % 2 == 0 else 4
                for c in range(NTT):
                    n = gn[c]
                    first_chunk = (n == 0)
                    bcur = bands[:, g * 3 + (2 if first_chunk else 0), :]
                    sc.op("pe", lambda e, c=c, bcur=bcur, fc=first_chunk, n=n: e.matmul(
                        out=psum[bank][:, c * 128:(c + 1) * 128], lhsT=up_tok[:, n % 5, g * 128:(g + 1) * 128],
                        rhs=bcur, start=True, stop=fc),
                        reads=[B_up[n % 5], B_cbf], writes=[B_P[bank]])
                    if not first_chunk:
                        sc.op("pe", lambda e, c=c, n=n: e.matmul(
                            out=psum[bank][:, c * 128:(c + 1) * 128],
                            lhsT=up_tok[:, (n - 1) % 5, g * 128:(g + 1) * 128],
                            rhs=bands[:, g * 3 + 1, :], start=False, stop=True),
                            reads=[B_up[(n - 1) % 5], B_cbf], writes=[B_P[bank]])
                sc.op("act", lambda e: e.activation(out=actT[:, 8 + g, :], in_=psum[bank][:], func=AF.Copy),
                      reads=[B_P[bank]], writes=[B_act[8 + g]])

            def pool2(g):
                bank2 = 6 + (g % 2)
                sc.op("pe", lambda e: e.matmul(
                    out=psum[bank2][:], lhsT=poolw_t[:, g, :], rhs=actT[:, 8 + g, :], start=True, stop=True),
                    reads=[B_cbf, B_act[8 + g]], writes=[B_P[bank2]])
                sc.op("act", lambda e: e.activation(
                    out=mixT[:, g, :], in_=psum[bank2][:], func=AF.Copy, scale=pscale_t[:, g:g + 1]),
                    reads=[B_P[bank2], B_const], writes=[B_act[g]])

            def wout_tt(tt):
                for dh in range(2):
                    bank = 4 + (tt % 2) * 2 + dh
                    for j in range(KC):
                        sl = wout_slots[2 * dh + j // 4]
                        wv = ring[sl].rearrange("p (k f) -> p k f", k=4)
                        sc.op("pe", lambda e, j=j, wv=wv, bank=bank: e.matmul(
                            out=psum[bank][:], lhsT=mixT[:, j, tt * 128:(tt + 1) * 128], rhs=wv[:, j % 4, :],
                            start=(j == 0), stop=(j == KC - 1)),
                            reads=[B_ring[sl], B_act[j]], writes=[B_P[bank]])
                for dh in range(2):
                    bank = 4 + (tt % 2) * 2 + dh
                    sc.op("dve", lambda e, dh=dh, bank=bank: e.tensor_tensor(
                        out=h[:, tt, dh * 512:(dh + 1) * 512], in0=psum[bank][:],
                        in1=h[:, tt, dh * 512:(dh + 1) * 512], op=ALU.add),
                        reads=[B_P[bank], B_h[hb][tt]], writes=[B_h[hb][tt]])

            def warm(n, bank):
                for _ in range(n):
                    sc.op("pe", lambda e: e.matmul(out=psum[bank][:], lhsT=ident[:],
                                                   rhs=bands[:].rearrange("p a b -> p (a b)")[:, 0:512],
                                                   start=True, stop=True),
                          reads=[B_cbf], writes=[B_P[bank]])

            proj(0)
            if tr3_pending:
                tr_tt(3, bank=7)
            proj(1)
            stage_a(0)
            proj(2)
            stage_a(1)
            if ORDER == 2:
                stage_b(0, bank=7)
                stage_b(1, bank=5)
                stage_c(0, bank=6)
            proj(3)
            for _ in range(10):
                issue_load()
            load_gain(2)
            stage_a(2)
            for g in range(4):
                pool1(g)
            if ORDER == 2:
                stage_a_tr(3)
                stage_b(2, bank=0)
                stage_c(1, bank=2)
                stage_b(3, bank=1)
                stage_c(2, bank=3)
                stage_d(0, dbank=0)
                for g in range(4):
                    pool2(g)
                stage_c(3, bank=2)
                stage_d(1, dbank=1)
                wout_tt(0)
                norm_tt(hb, 0)
                stage_d(2, dbank=0)
                wout_tt(1)
                norm_tt(hb, 1)
                stage_d(3, dbank=1)
                stage_a_kv(3, kvb=3)
                wout_tt(2)
                norm_tt(hb, 2)
                tr_tt(0)
                wout_tt(3)
                norm_tt(hb, 3)
                for _ in range(4):
                    issue_load()
                tr_tt(1)
                tr_tt(2, bank=2)
                tr_tt(3, bank=3)
                return
            stage_a(3)
            if ORDER == 0:
                stage_b(0)
                stage_b(1)
                stage_c(0)
                stage_b(2)
                stage_c(1)
                stage_d(0)
                stage_b(3)
                stage_c(2)
                stage_d(1)
                for g in range(4):
                    pool2(g)
                stage_c(3)
                stage_d(2)
                stage_d(3)
                for tt in range(NTT):
                    wout_tt(tt)
                    norm_tt(hb, tt)
                    if tt >= 1:
                        tr_tt(tt - 1)
                for _ in range(4):
                    issue_load()
                tr_tt(3)
            else:
                stage_b(0)
                stage_b(1)
                warm(WARM[0], 2)
                stage_c(0)
                stage_b(2)
                warm(WARM[1], 3)
                stage_c(1)
                stage_b(3)
                warm(WARM[2], 2)
                stage_c(2)
                stage_d(0)
                for g in range(4):
                    pool2(g)
                stage_c(3)
                stage_d(1)
                wout_tt(0)
                norm_tt(hb, 0)
                stage_d(2)
                wout_tt(1)
                norm_tt(hb, 1)
                stage_d(3)
                wout_tt(2)
                norm_tt(hb, 2)
                tr_tt(0)
                wout_tt(3)
                norm_tt(hb, 3)
                for _ in range(4):
                    issue_load()
                tr_tt(1)
                warm(WARM[3], 2)
                tr_tt(2, bank=2)
                tr_tt(3, bank=3)
                warm(8, 0)

        def emit_final(ti, hb, tts=range(NTT)):
            h = hbuf[hb]
            for tt in tts:
                i = 4 + tt
                sc.op("act", lambda e, tt=tt, i=i: e.activation(out=junk, in_=h[:, tt, :], func=AF.Square,
                                                              accum_out=ssq[:, i:i + 1]),
                      reads=[B_h[hb][tt]], writes=[B_ssq[i]])
                sc.op("act", lambda e, i=i: e.activation(out=rt[:, i:i + 1], in_=ssq[:, i:i + 1], func=AF.Sqrt,
                                                       bias=eps_t[:], scale=1.0 / D),
                      reads=[B_ssq[i], B_eps], writes=[B_rt[i]])
                sc.op("dve", lambda e, i=i: e.reciprocal(out=rstd[:, i:i + 1], in_=rt[:, i:i + 1]),
                      reads=[B_rt[i]], writes=[B_rstd[i]])
                sc.op("dve", lambda e, tt=tt, i=i: e.scalar_tensor_tensor(
                    out=h[:, tt, :], in0=h[:, tt, :], scalar=rstd[:, i:i + 1], in1=fgain_b[:],
                    op0=ALU.mult, op1=ALU.mult),
                    reads=[B_h[hb][tt], B_rstd[i], B_const], writes=[B_h[hb][tt]])
                store_y(ti, tt)

        load_x(0)
        load_rope(0)
        load_gain(0)
        for _ in range(NS):
            issue_load()
        for tt in range(NTT):
            norm_tt(0, tt)
        for tt in range(NTT):
            tr_tt(tt)
        for ti in range(ntiles):
            hb = ti % 2
            if ti + 1 < ntiles:
                load_x(ti + 1)
                load_rope(ti + 1)
            ffn_phase_a(ti, 0)
            load_gain(1)
            for tt in range(NTT):
                ffn_phase_b_tt(ti, hb, 0, tt)
                norm_tt(hb, tt)
                if tt >= 1:
                    tr_tt(tt - 1, bank=(4 if tt == 3 else None))
            for _ in range(NG):
                issue_load()
            mixer(ti, hb, True)
            ffn_phase_a(ti, 47)
            nxt = ti + 1 < ntiles
            if nxt:
                load_gain(0)
                for tt in range(NTT):
                    norm_tt(1 - hb, tt)
            for tt in range(NTT):
                ffn_phase_b_tt(ti, hb, 47, tt)
                if nxt and tt < 2:
                    tr_tt(2 * tt)
                    tr_tt(2 * tt + 1)
                emit_final(ti, hb, [tt])
            for _ in range(NG):
                issue_load()
        sc.wait_only("sp", out_ops)
        sc.wait_only("pool", out_ops)

        sc.finalize(engsem)
        with nc.Block() as block:
            @block.tensor
            def _(e):
                sc.emit("pe", e)

            @block.scalar
            def _(e):
                sc.emit("act", e)

            @block.vector
            def _(e):
                sc.emit("dve", e)

            @block.gpsimd
            def _(e):
                sc.emit("pool", e)

            @block.sync
            def _(e):
                sc.emit("sp", e)
    return nc


def make_consts():
    ident = np.eye(128, dtype=np.float32)
    windows = (2, 4, 8, 16)
    bands = np.zeros((128, 12, 128), np.float32)
    s = np.arange(128)[:, None]
    t = np.arange(128)[None, :]
    for g, w in enumerate(windows):
        cur = ((s <= t) & (s > t - w)).astype(np.float32) / w - (s == t).astype(np.float32)
        prev = ((s - 128 <= t) & (s - 128 > t - w)).astype(np.float32) / w
        cnt = np.minimum(t + 1, w).astype(np.float32)
        first = ((s <= t) & (s > t - w)).astype(np.float32) / cnt - (s == t).astype(np.float32)
        bands[:, g * 3 + 0, :] = cur
        bands[:, g * 3 + 1, :] = prev
        bands[:, g * 3 + 2, :] = first
    mask = (t >= s).astype(np.float32)
    hh = np.arange(H, dtype=np.float64)
    log_gamma = np.log1p(-np.exp2(-5.0 - hh))
    pos = np.arange(C, dtype=np.float64)
    qfac = np.exp(log_gamma[None, :] * (pos[:, None] + 1.0))
    kfac = np.exp(-log_gamma[None, :] * (pos[:, None] + 1.0)) * (128 ** -0.5)
    fac = np.concatenate([qfac, kfac], axis=1).astype(np.float32)
    gc = np.repeat(np.exp(log_gamma * C)[None, :, None], 128, axis=0)
    gc = np.repeat(gc, 128, axis=2).reshape(128, 512).astype(np.float32)
    inv_freq = 1.0 / (10000.0 ** (np.arange(0, 128, 2, dtype=np.float32) / np.float32(128)))
    ang = np.arange(S, dtype=np.float32)[:, None] * inv_freq[None, :].astype(np.float32)
    rope = np.concatenate([np.cos(ang), np.sin(ang)], axis=1).astype(np.float32)
    return dict(c_ident=ident, c_bands=bands, c_mask=mask, c_fac=fac, c_gc=gc, c_rope=rope)


_NC_CACHE = {}


def kernel(x, ffn1_norm, ffn1_w1, ffn1_w3, ffn1_w2, mix_norm, w_in, pool_w, pool_scale,
           ret_norm, w_out, ffn2_norm, ffn2_w1, ffn2_w3, ffn2_w2, final_norm, _ntiles=S // T):
    f = lambda a: np.ascontiguousarray(np.asarray(a, dtype=np.float32))
    x = f(x)
    B = x.shape[0]
    ntok = _ntiles * T
    shared = dict(
        w1a=f(ffn1_w1)[0], w3a=f(ffn1_w3)[0], w2a=f(ffn1_w2)[0],
        w1b=f(ffn2_w1)[0], w3b=f(ffn2_w3)[0], w2b=f(ffn2_w2)[0],
        w_in=f(w_in)[0], w_out=f(w_out)[0], pool_w=f(pool_w)[0],
        gains=np.ascontiguousarray(np.stack([f(ffn1_norm)[0], f(mix_norm)[0], f(ffn2_norm)[0]])),
        fgain=f(final_norm), retg=f(ret_norm)[0],
        pscale=np.ascontiguousarray(f(pool_scale)[0].reshape(4, 128).T),
    )
    shared.update(make_consts())
    if _ntiles not in _NC_CACHE:
        _NC_CACHE[_ntiles] = build_nc(_ntiles)
    nc = _NC_CACHE[_ntiles]
    in_maps = []
    for b in range(B):
        m = dict(shared)
        m["x"] = np.ascontiguousarray(x[b, :ntok])
        in_maps.append(m)
    res = run_bass_kernel_spmd(nc, in_maps, core_ids=list(range(B)))
    return np.stack([np.asarray(r["y"], dtype=np.float32) for r in res.results], axis=0)
```

```python
import numpy as np
from contextlib import ExitStack
import concourse.bass as bass
import concourse.mybir as mybir
from concourse.bass_utils import run_bass_kernel_spmd

F32 = mybir.dt.float32
BF = mybir.dt.bfloat16
AF = mybir.ActivationFunctionType
ALU = mybir.AluOpType

D = 1024
S = 4096
DFF = 2816
T = 512
NTT = 4
KC = 8
NF = 22
NG = 11
H = 4
C = 128
EPS = 1e-6
NSTREAM = 80
NS = 16
ENGS = ("pe", "act", "dve", "pool", "sp")
WARM = [8, 8, 0, 8]
CMERGE = 0
ORDER = 1


class Buf:
    __slots__ = ("w", "r", "rd", "name", "const")

    def __init__(self, name="", const=False):
        self.w = None
        self.r = {}
        self.rd = []
        self.name = name
        self.const = const


class Op:
    __slots__ = ("eng", "fn", "deps", "needed", "val", "sem", "ndma", "seq")

    def __init__(self, eng, fn):
        self.eng = eng
        self.fn = fn
        self.deps = []
        self.needed = False
        self.val = None
        self.sem = None
        self.ndma = 0
        self.seq = 0


class Sched:
    def __init__(self):
        self.ops = {e: [] for e in ENGS}
        self.dma_cum = {}

    def _track(self, o, reads, writes):
        o.seq = len(self.ops[o.eng])
        cand = []
        for b in reads:
            if b.w is not None:
                cand.append(b.w)
        for b in writes:
            if b.w is not None:
                cand.append(b.w)
            cand.extend(b.r.values())
            cand.extend(b.rd)
        best = {}
        for d in cand:
            if d is o:
                continue
            if d.ndma:
                key = ("dma", d.sem.num)
                if key not in best or best[key].val < d.val:
                    best[key] = d
            else:
                if d.eng == "pe" and o.eng == "pe":
                    continue
                key = ("eng", d.eng)
                if key not in best or best[key].seq < d.seq:
                    best[key] = d
        for d in best.values():
            o.deps.append(d)
            d.needed = True
        for b in reads:
            if b.const:
                continue
            if o.ndma:
                b.rd.append(o)
            else:
                b.r[o.eng] = o
        for b in writes:
            b.w = o
            b.r = {}
            b.rd = []

    def op(self, eng, fn, reads=(), writes=()):
        o = Op(eng, fn)
        self._track(o, reads, writes)
        self.ops[eng].append(o)
        return o

    def dma(self, eng, fn, sem, n, reads=(), writes=()):
        o = Op(eng, fn)
        o.ndma = n
        o.sem = sem
        key = sem.num
        self.dma_cum[key] = self.dma_cum.get(key, 0) + 16 * n
        o.val = self.dma_cum[key]
        self._track(o, reads, writes)
        self.ops[eng].append(o)
        return o

    def wait_only(self, eng, deps):
        o = Op(eng, None)
        for d in deps:
            o.deps.append(d)
            d.needed = True
        self.ops[eng].append(o)
        return o

    def finalize(self, engsem):
        for e in ENGS:
            cnt = 0
            for o in self.ops[e]:
                if o.ndma:
                    continue
                if o.needed and o.fn is not None:
                    cnt += 1
                    o.val = cnt
                    o.sem = engsem[e]

    def emit(self, eng, h):
        waited = {}
        for o in self.ops[eng]:
            need = {}
            for d in o.deps:
                k = d.sem.num
                if waited.get(k, 0) >= d.val:
                    continue
                if k not in need or need[k][1] < d.val:
                    need[k] = (d.sem, d.val)
            for k, (sem, val) in need.items():
                h.wait_ge(sem, val)
                waited[k] = val
            if o.fn is None:
                continue
            r = o.fn(h)
            if o.ndma:
                assert len(r) == o.ndma
                for ins in r:
                    ins.then_inc(o.sem, 16)
            elif o.needed:
                r.then_inc(o.sem, 1)


def build_nc(ntiles=S // T):
    nc = bass.Bass("TRN2", target_bir_lowering=False)
    ntok = ntiles * T

    def din(name, shape, dt=F32):
        return nc.dram_tensor(name, list(shape), dt, kind="ExternalInput").ap()

    x = din("x", [ntok, D])
    y = nc.dram_tensor("y", [ntok, D], F32, kind="ExternalOutput").ap()
    w1 = [din("w1a", [D, DFF]), din("w1b", [D, DFF])]
    w3 = [din("w3a", [D, DFF]), din("w3b", [D, DFF])]
    w2 = [din("w2a", [DFF, D]), din("w2b", [DFF, D])]
    w_in = din("w_in", [D, 2560])
    w_out = din("w_out", [D, D])
    pool_w = din("pool_w", [4, 128, 128])
    gains = din("gains", [3, D])
    fgain = din("fgain", [D])
    retg = din("retg", [512])
    pscale = din("pscale", [128, 4])
    c_ident = din("c_ident", [128, 128])
    c_bands = din("c_bands", [128, 12, 128])
    c_mask = din("c_mask", [128, 128])
    c_fac = din("c_fac", [128, 8])
    c_gc = din("c_gc", [128, 512])
    c_rope = din("c_rope", [S, 128])
    scr = nc.dram_tensor("scr", [NSTREAM, 128, 2048], BF, kind="Internal").ap()

    es = ExitStack()
    with es:
        def sb(name, shape, dt):
            return es.enter_context(nc.sbuf_tensor(name, list(shape), dt))

        def sem(name):
            return es.enter_context(nc.semaphore(name))

        hbuf = [sb(f"h{i}", [128, NTT, D], F32) for i in range(2)]
        xnb = sb("xnb", [128, NTT, D], BF)
        xnT = sb("xnT", [128, KC, T], BF)
        actT = sb("actT", [128, NF, T], BF)
        ring_all = sb("ring_all", [128, NS, 2048], BF)
        ring = [ring_all[:, i, :] for i in range(NS)]
        silu_t = [sb(f"silu{i}", [128, T], F32) for i in range(2)]
        gain_cur = sb("gain_cur", [128, D], F32)
        fgain_b = sb("fgain_b", [128, D], F32)
        retg_b = sb("retg_b", [128, 512], F32)
        pscale_t = sb("pscale_t", [128, 4], F32)
        rope_t = [sb(f"rope{i}", [128, NTT, 128], F32) for i in range(2)]
        ident = sb("ident", [128, 128], BF)
        bands = sb("bands", [128, 12, 128], BF)
        mask_t = sb("mask_t", [128, 128], F32)
        fac_t = sb("fac_t", [128, 8], F32)
        gc_t = sb("gc_t", [128, 512], F32)
        eps_t = sb("eps_t", [128, 1], F32)
        poolw_t = sb("poolw_t", [128, 4, 128], BF)
        ssq = sb("ssq", [128, 8], F32)
        rt = sb("rt", [128, 8], F32)
        rstd = sb("rstd", [128, 8], F32)
        up_tok = sb("up_tok", [128, 5, 512], BF)
        v_tok = sb("v_tok", [128, 4, 512], BF)
        sg32 = sb("sg32", [128, 512], F32)
        sgb = sb("sgb", [128, 4, 512], BF)
        rB = sb("rB", [128, 2, 512], F32)
        qrot = sb("qrot", [128, 2, 512], BF)
        krot = sb("krot", [128, 2, 512], BF)
        qkT = sb("qkT", [128, 4, 2, 512], BF)
        PT = sb("PT", [128, 2, 512], BF)
        R32 = sb("R32", [128, 512], F32)
        Rbf = sb("Rbf", [128, 5, 512], BF)
        btok = sb("btok", [128, 4, 512], BF)
        ossq = sb("ossq", [128, 2, 4], F32)
        ort = sb("ort", [128, 2, 4], F32)
        orinv = sb("orinv", [128, 2, 4], F32)
        junk = rB[:].rearrange("p a b -> p (a b)").bitcast(BF)[:, 0:D]
        mixT = actT
        f32v = actT[:, 12:22, :].rearrange("p a b -> p (a b)").bitcast(F32)
        qs32 = [f32v[:, 0:512], f32v[:, 512:1024]]
        ks32 = [f32v[:, 1024:1536], f32v[:, 1536:2048]]
        tmpKV = f32v[:, 2048:2560]

        psum = [es.enter_context(nc.psum_tensor(f"ps{i}", [128, 512], F32)) for i in range(8)]

        engsem = {e: sem(f"sem_{e}") for e in ("pe", "act", "dve", "pool")}
        ring_sem = [sem(f"ring{i}") for i in range(NS)]
        ring_sw_sem = [sem(f"ringsw{i}") for i in range(NS)]
        fill_sem = [sem(f"fill{i}") for i in range(NSTREAM // 2)]
        x_sem = [sem("x0"), sem("x1")]
        o_sem = [sem("o0"), sem("o1")]
        rope_sem = [sem("rp0"), sem("rp1")]
        c_sem = sem("consts")
        pw_sem = sem("poolw")
        g_sem = sem("gain")

        sc = Sched()

        B_h = [[Buf(f"h{b}_{t}") for t in range(NTT)] for b in range(2)]
        B_xnb = [Buf() for _ in range(NTT)]
        B_xnT = [Buf() for _ in range(NTT)]
        B_act = [Buf(f"act{f}") for f in range(NF)]
        B_ring = [Buf(f"ring{i}") for i in range(NS)]
        B_scr = [Buf() for _ in range(NSTREAM)]
        B_P = [Buf(f"P{i}") for i in range(8)]
        B_silu = [Buf(), Buf()]
        B_const = Buf("const", const=True)
        B_cbf = Buf("cbf", const=True)
        B_gain = Buf("gain")
        B_rope = [Buf(), Buf()]
        B_ssq = [Buf() for _ in range(8)]
        B_rt = [Buf() for _ in range(8)]
        B_rstd = [Buf() for _ in range(8)]
        B_up = [Buf() for _ in range(5)]
        B_v = [Buf() for _ in range(4)]
        B_sg32 = Buf()
        B_sgb = [Buf() for _ in range(4)]
        B_rB = [Buf(), Buf()]
        B_qrot = [Buf(), Buf()]
        B_krot = [Buf(), Buf()]
        B_qkT = [Buf() for _ in range(4)]
        B_PT = [Buf(), Buf()]
        B_R32 = Buf()
        B_Rbf = [Buf() for _ in range(5)]
        B_btok = [Buf() for _ in range(4)]
        B_ossq = [[Buf() for _ in range(4)] for _ in range(2)]
        B_ort = [Buf(), Buf()]
        B_orinv = [Buf(), Buf()]
        Bq32 = [B_act[12 + 0], B_act[12 + 1]], [B_act[12 + 2], B_act[12 + 3]]
        Bk32 = [B_act[12 + 4], B_act[12 + 5]], [B_act[12 + 6], B_act[12 + 7]]
        BtKV = [B_act[20], B_act[21]]

        def stream_src(k):
            def blk(w, half, G):
                return w[half * 512:(half + 1) * 512, G * 512:(G + 1) * 512].rearrange("(k p) f -> p k f", p=128)

            def colslot(w, c0):
                return w[:, c0:c0 + 256].rearrange("(k p) f -> p k f", p=128)

            def rowslot(w, j):
                return w[j * 256:(j + 1) * 256, :].rearrange("(a p) d -> p a d", p=128)

            if k < 33:
                ff, kk = 0, k
            elif k < 47:
                kk = k - 33
                if kk < 10:
                    return blk(w_in, kk % 2, kk // 2)
                kk -= 10
                return blk(w_out, kk % 2, kk // 2)
            else:
                ff, kk = 1, k - 47
            if kk < 20:
                G, r = kk // 4, kk % 4
                return blk(w1[ff] if r < 2 else w3[ff], r % 2, G)
            if kk < 22:
                return colslot(w1[ff] if kk == 20 else w3[ff], 2560)
            return rowslot(w2[ff], kk - 22)

        def const_loads(e):
            r = []
            r.append(e.dma_start(out=mask_t[:], in_=c_mask))
            r.append(e.dma_start(out=fac_t[:], in_=c_fac))
            r.append(e.dma_start(out=gc_t[:], in_=c_gc))
            r.append(e.dma_start(out=pscale_t[:], in_=pscale))
            r.append(e.dma_start(out=fgain_b[:], in_=fgain.partition_broadcast(128)))
            r.append(e.dma_start(out=retg_b[:], in_=retg.partition_broadcast(128)))
            return r

        sc.dma("sp", const_loads, c_sem, 6, writes=[B_const])

        def cast_consts(e):
            return [e.dma_start(out=ident[:], in_=c_ident),
                    e.dma_start(out=bands[:], in_=c_bands),
                    e.dma_start(out=poolw_t[:], in_=pool_w.rearrange("g c e -> c g e"))]

        sc.dma("pool", cast_consts, pw_sem, 3, writes=[B_cbf])
        B_eps = Buf("eps", const=True)
        sc.op("dve", lambda e: e.memset(eps_t[:], EPS), writes=[B_eps])
        DIRECT0 = 1
        if not DIRECT0:
            for k in range(NSTREAM):
                def fill(e, k=k):
                    src = stream_src(k)
                    dst = scr[k].rearrange("p (a b) -> p a b", a=src.shape[1])
                    return [e.dma_start(out=dst, in_=src)]
                sc.dma("pool", fill, fill_sem[k // 2], 1, writes=[B_scr[k]])
            for k in range(0, NSTREAM, 2):
                B_scr[k].w = B_scr[k + 1].w
        sc.op("pool", lambda e: e.memset(R32[:], 0.0), writes=[B_R32])
        sc.op("pool", lambda e: e.memset(Rbf[:, 0, :], 0.0), writes=[B_Rbf[0]])

        state = {"next_load": 0, "total": ntiles * NSTREAM}

        def issue_load():
            n = state["next_load"]
            if n >= state["total"]:
                return
            state["next_load"] = n + 1
            s = n % NS
            k = n % NSTREAM
            SPREAD = 4
            tq = n // NSTREAM
            kq = k // (NSTREAM // SPREAD)
            if DIRECT0 and kq >= tq:
                src = stream_src(k)
                dst = ring[s].rearrange("p (a b) -> p a b", a=src.shape[1])
                sc.dma("pool", lambda e: [e.dma_start(out=dst, in_=src)], ring_sw_sem[s], 1, writes=[B_ring[s]])
                if kq == tq:
                    so = sc.dma("sp", lambda e: [e.dma_start(out=scr[k], in_=ring[s])], fill_sem[k // 2], 1,
                                reads=[B_ring[s]], writes=[B_scr[k]])
                    if k % 2 == 0:
                        state["even_store"] = so
                    else:
                        state["even_store"].val = so.val
                return
            sc.dma("sp", lambda e: [e.dma_start(out=ring[s], in_=scr[k])], ring_sem[s], 1,
                   reads=[B_scr[k]], writes=[B_ring[s]])

        def slot_of(ti, k):
            return (ti * NSTREAM + k) % NS

        def load_x(ti):
            b = ti % 2
            src = x[ti * T:(ti + 1) * T, :].rearrange("(t p) d -> p t d", p=128)
            sc.dma("sp", lambda e: [e.dma_start(out=hbuf[b][:], in_=src)], x_sem[b], 1, writes=B_h[b])

        def load_rope(ti):
            b = ti % 2
            src = c_rope[ti * T:(ti + 1) * T, :].rearrange("(t p) d -> p t d", p=128)
            sc.dma("sp", lambda e: [e.dma_start(out=rope_t[b][:], in_=src)], rope_sem[b], 1, writes=[B_rope[b]])

        def load_gain(gi):
            sc.dma("sp", lambda e: [e.dma_start(out=gain_cur[:], in_=gains[gi].partition_broadcast(128))],
                   g_sem, 1, writes=[B_gain])

        out_ops = []

        def store_y(ti, tt):
            b = ti % 2
            dst = y[ti * T + tt * 128: ti * T + (tt + 1) * 128, :]
            out_ops.append(sc.dma("pool", lambda e: [e.dma_start(out=dst, in_=hbuf[b][:, tt, :])], o_sem[b], 1,
                                  reads=[B_h[b][tt]]))

        def norm_tt(hb, tt):
            h = hbuf[hb]
            sc.op("act", lambda e: e.activation(out=xnb[:, tt, :], in_=h[:, tt, :], func=AF.Square,
                                                accum_out=ssq[:, tt:tt + 1]),
                  reads=[B_h[hb][tt]], writes=[B_xnb[tt], B_ssq[tt]])
            sc.op("act", lambda e: e.activation(out=rt[:, tt:tt + 1], in_=ssq[:, tt:tt + 1], func=AF.Sqrt,
                                                bias=eps_t[:], scale=1.0 / D),
                  reads=[B_ssq[tt], B_eps], writes=[B_rt[tt]])
            sc.op("dve", lambda e: e.reciprocal(out=rstd[:, tt:tt + 1], in_=rt[:, tt:tt + 1]),
                  reads=[B_rt[tt]], writes=[B_rstd[tt]])
            sc.op("dve", lambda e: e.scalar_tensor_tensor(
                out=xnb[:, tt, :], in0=h[:, tt, :], scalar=rstd[:, tt:tt + 1], in1=gain_cur[:],
                op0=ALU.mult, op1=ALU.mult),
                reads=[B_h[hb][tt], B_rstd[tt], B_gain], writes=[B_xnb[tt]])

        def tr_tt(tt, bank=None):
            if bank is None:
                bank = tt % 2
            pbf = psum[bank][:].bitcast(BF)
            for kc in range(KC):
                sc.op("pe", lambda e, kc=kc: e.transpose(
                    out=pbf[:, kc * 128:(kc + 1) * 128], in_=xnb[:, tt, kc * 128:(kc + 1) * 128], identity=ident[:]),
                    reads=[B_xnb[tt], B_cbf], writes=[B_P[bank]])
            src = pbf.rearrange("p (k t) -> p k t", k=KC)
            dst = xnT[:, :, tt * 128:(tt + 1) * 128]
            if tt % 2 == 0:
                sc.op("act", lambda e: e.activation(out=dst, in_=src, func=AF.Copy),
                      reads=[B_P[bank]], writes=[B_xnT[tt]])
            else:
                sc.op("dve", lambda e: e.tensor_copy(out=dst, in_=src),
                      reads=[B_P[bank]], writes=[B_xnT[tt]])

        def ffn_phase_a(ti, kbase):
            def fchunk(f, j, w1s, w3s, nk):
                pa, pb = (0, 1) if f % 2 == 0 else (2, 3)
                for (ws, pbk) in ((w1s, pa), (w3s, pb)):
                    for kc in range(KC):
                        sl, wv = ws[kc // nk]
                        sc.op("pe", lambda e, wv=wv, kc=kc, pbk=pbk: e.matmul(
                            out=psum[pbk][:], lhsT=wv[:, kc % nk, j * 128:(j + 1) * 128], rhs=xnT[:, kc, :],
                            start=(kc == 0), stop=(kc == KC - 1)),
                            reads=[B_ring[sl]] + B_xnT, writes=[B_P[pbk]])
                st = f % 2
                sc.op("act", lambda e: e.activation(out=silu_t[st][:], in_=psum[pa][:], func=AF.Silu),
                      reads=[B_P[pa]], writes=[B_silu[st]])
                sc.op("dve", lambda e: e.tensor_tensor(
                    out=actT[:, f, :], in0=silu_t[st][:], in1=psum[pb][:], op=ALU.mult),
                    reads=[B_silu[st], B_P[pb]], writes=[B_act[f]])

            for G in range(5):
                sl = [slot_of(ti, kbase + 4 * G + i) for i in range(4)]
                vw = [(x, ring[x].rearrange("p (k f) -> p k f", k=4)) for x in sl]
                for j in range(4):
                    fchunk(4 * G + j, j, vw[0:2], vw[2:4], 4)
                for _ in range(4):
                    issue_load()
            s1 = slot_of(ti, kbase + 20)
            s3 = slot_of(ti, kbase + 21)
            v1 = [(s1, ring[s1].rearrange("p (k f) -> p k f", k=KC))]
            v3 = [(s3, ring[s3].rearrange("p (k f) -> p k f", k=KC))]
            for j in range(2):
                fchunk(20 + j, j, v1, v3, 8)
            issue_load()
            issue_load()

        def ffn_phase_b_tt(ti, hb, kbase, tt):
            h = hbuf[hb]
            for dh in range(2):
                bank = 4 + (tt % 2) * 2 + dh
                for f in range(NF):
                    sl = slot_of(ti, kbase + 22 + f // 2)
                    wv = ring[sl].rearrange("p (a d) -> p a d", a=2)
                    sc.op("pe", lambda e, wv=wv, f=f, dh=dh, bank=bank: e.matmul(
                        out=psum[bank][:], lhsT=actT[:, f, tt * 128:(tt + 1) * 128],
                        rhs=wv[:, f % 2, dh * 512:(dh + 1) * 512], start=(f == 0), stop=(f == NF - 1)),
                        reads=[B_ring[sl], B_act[f]], writes=[B_P[bank]])
                sc.op("dve", lambda e, dh=dh, bank=bank: e.scalar_tensor_tensor(
                    out=h[:, tt, dh * 512:(dh + 1) * 512], in0=psum[bank][:], scalar=0.5,
                    in1=h[:, tt, dh * 512:(dh + 1) * 512], op0=ALU.mult, op1=ALU.add),
                    reads=[B_P[bank], B_h[hb][tt]], writes=[B_h[hb][tt]])

        def mixer(ti, hb, tr3_pending):
            h = hbuf[hb]
            rp = rope_t[ti % 2]
            B_rp = B_rope[ti % 2]
            kbase = 33
            win_slots = [slot_of(ti, kbase + i) for i in range(10)]
            wout_slots = [slot_of(ti, kbase + 10 + i) for i in range(4)]
            gn = [ti * 4 + c for c in range(4)]

            def rope(eng, X32, Bsrc, ri, dst, Bdst, c):
                X = X32.rearrange("p (h t d) -> p h t d", h=H, t=2)
                Bv = rB[:, ri, :].rearrange("p (h t d) -> p h t d", h=H, t=2)
                O = dst.rearrange("p (h t d) -> p h t d", h=H, t=2)
                cos4 = rp[:, c, 0:64].unsqueeze(1).unsqueeze(1).broadcast_to([128, H, 2, 64])
                sin3 = rp[:, c, 64:128].unsqueeze(1).broadcast_to([128, H, 64])
                sc.op(eng, lambda e: e.tensor_tensor(out=Bv[:, :, 0, :], in0=X[:, :, 1, :], in1=sin3, op=ALU.mult),
                      reads=Bsrc + [B_rp], writes=[B_rB[ri]])
                sc.op(eng, lambda e: e.tensor_tensor(out=Bv[:, :, 1, :], in0=X[:, :, 0, :], in1=sin3, op=ALU.mult),
                      reads=Bsrc + [B_rp], writes=[B_rB[ri]])
                sc.op(eng, lambda e: e.tensor_tensor(out=X, in0=X, in1=cos4, op=ALU.mult),
                      reads=Bsrc + [B_rp], writes=Bsrc)
                sc.op(eng, lambda e: e.tensor_tensor(out=O[:, :, 0, :], in0=X[:, :, 0, :], in1=Bv[:, :, 0, :],
                                                     op=ALU.subtract),
                      reads=Bsrc + [B_rB[ri]], writes=[Bdst])
                sc.op(eng, lambda e: e.tensor_tensor(out=O[:, :, 1, :], in0=X[:, :, 1, :], in1=Bv[:, :, 1, :],
                                                     op=ALU.add),
                      reads=Bsrc + [B_rB[ri]], writes=[Bdst])

            def proj(c):
                n = gn[c]
                p2 = c % 2
                for bi in (1, 2, 3, 4, 0):
                    for kc in range(KC):
                        sl = win_slots[bi * 2 + kc // 4]
                        wv = ring[sl].rearrange("p (k f) -> p k f", k=4)
                        sc.op("pe", lambda e, kc=kc, bi=bi, wv=wv: e.matmul(
                            out=psum[bi][:], lhsT=xnT[:, kc, c * 128:(c + 1) * 128], rhs=wv[:, kc % 4, :],
                            start=(kc == 0), stop=(kc == KC - 1)),
                            reads=[B_ring[sl], B_xnT[c]], writes=[B_P[bi]])
                    if bi == 0:
                        sc.op("act", lambda e: e.activation(out=up_tok[:, n % 5, :], in_=psum[0][:], func=AF.Copy),
                              reads=[B_P[0]], writes=[B_up[n % 5]])
                    elif bi == 1:
                        for hh in range(H):
                            sc.op("act", lambda e, hh=hh: e.activation(
                                out=qs32[p2][:, hh * 128:(hh + 1) * 128], in_=psum[1][:, hh * 128:(hh + 1) * 128],
                                func=AF.Copy, scale=fac_t[:, hh:hh + 1]),
                                reads=[B_P[1], B_const], writes=[Bq32[p2][hh // 2]])
                        rope("pool", qs32[p2], Bq32[p2], 0, qrot[:, p2, :], B_qrot[p2], c)
                    elif bi == 2:
                        for hh in range(H):
                            sc.op("act", lambda e, hh=hh: e.activation(
                                out=ks32[p2][:, hh * 128:(hh + 1) * 128], in_=psum[2][:, hh * 128:(hh + 1) * 128],
                                func=AF.Copy, scale=fac_t[:, 4 + hh:5 + hh]),
                                reads=[B_P[2], B_const], writes=[Bk32[p2][hh // 2]])
                        rope("dve", ks32[p2], Bk32[p2], 1, krot[:, p2, :], B_krot[p2], c)
                    elif bi == 3:
                        sc.op("act", lambda e: e.activation(out=v_tok[:, c, :], in_=psum[3][:], func=AF.Copy),
                              reads=[B_P[3]], writes=[B_v[c]])
                    else:
                        sc.op("act", lambda e: e.activation(out=sg32[:], in_=psum[4][:], func=AF.Silu),
                              reads=[B_P[4]], writes=[B_sg32])
                        sc.op("pool", lambda e: e.tensor_tensor(out=sgb[:, c, :], in0=sg32[:], in1=retg_b[:],
                                                                op=ALU.mult),
                              reads=[B_sg32, B_const], writes=[B_sgb[c]])

            def stage_a_tr(c):
                n = gn[c]
                p2 = c % 2
                pbf = psum[5][:].bitcast(BF)
                for qi, (srcT, Bs) in enumerate(((qrot, B_qrot[p2]), (krot, B_krot[p2]))):
                    for hh in range(H):
                        sc.op("pe", lambda e, qi=qi, hh=hh, srcT=srcT: e.transpose(
                            out=pbf[:, qi * 512 + hh * 128: qi * 512 + (hh + 1) * 128],
                            in_=srcT[:, p2, hh * 128:(hh + 1) * 128], identity=ident[:]),
                            reads=[Bs, B_cbf], writes=[B_P[5]])
                sc.op("dve", lambda e: e.tensor_copy(out=qkT[:, c, :, :].rearrange("p a b -> p (a b)"), in_=pbf),
                      reads=[B_P[5]], writes=[B_qkT[c]])

            def stage_a_kv(c, kvb=6):
                n = gn[c]
                p2 = c % 2
                for hh in range(H):
                    sc.op("pe", lambda e, hh=hh: e.matmul(
                        out=psum[kvb][:, hh * 128:(hh + 1) * 128], lhsT=krot[:, p2, hh * 128:(hh + 1) * 128],
                        rhs=v_tok[:, c, hh * 128:(hh + 1) * 128], start=True, stop=True),
                        reads=[B_krot[p2], B_v[c]], writes=[B_P[kvb]])
                sc.op("dve", lambda e: e.tensor_tensor(out=tmpKV, in0=psum[kvb][:], in1=R32[:], op=ALU.add),
                      reads=[B_P[kvb], B_R32], writes=BtKV)
                sc.op("dve", lambda e: e.tensor_tensor(out=R32[:], in0=tmpKV, in1=gc_t[:], op=ALU.mult),
                      reads=BtKV + [B_const], writes=[B_R32])
                sc.op("dve", lambda e: e.tensor_copy(out=Rbf[:, (n + 1) % 5, :], in_=R32[:]),
                      reads=[B_R32], writes=[B_Rbf[(n + 1) % 5]])


            def stage_a(c):
                stage_a_tr(c)
                stage_a_kv(c)

            def stage_b(c, bank=None):
                if bank is None:
                    bank = c % 2
                p2 = c % 2
                for hh in range(H):
                    sc.op("pe", lambda e, hh=hh: e.matmul(
                        out=psum[bank][:, hh * 128:(hh + 1) * 128], lhsT=qkT[:, c, 1, hh * 128:(hh + 1) * 128],
                        rhs=qkT[:, c, 0, hh * 128:(hh + 1) * 128], start=True, stop=True),
                        reads=[B_qkT[c]], writes=[B_P[bank]])
                m4 = mask_t[:].unsqueeze(1).broadcast_to([128, H, 128])
                sc.op("dve", lambda e: e.tensor_tensor(
                    out=PT[:, p2, :].rearrange("p (h c) -> p h c", h=H),
                    in0=psum[bank][:].rearrange("p (h c) -> p h c", h=H), in1=m4, op=ALU.mult),
                    reads=[B_P[bank], B_const], writes=[B_PT[p2]])

            def stage_c_old(c, bank=None):
                n = gn[c]
                if bank is None:
                    bank = 2 + c % 2
                p2 = c % 2
                for hh in range(H):
                    sc.op("pe", lambda e, hh=hh: e.matmul(
                        out=psum[bank][:, hh * 128:(hh + 1) * 128], lhsT=PT[:, p2, hh * 128:(hh + 1) * 128],
                        rhs=v_tok[:, c, hh * 128:(hh + 1) * 128], start=True, stop=False),
                        reads=[B_PT[p2], B_v[c]], writes=[B_P[bank]])
                    sc.op("pe", lambda e, hh=hh: e.matmul(
                        out=psum[bank][:, hh * 128:(hh + 1) * 128], lhsT=qkT[:, c, 0, hh * 128:(hh + 1) * 128],
                        rhs=Rbf[:, n % 5, hh * 128:(hh + 1) * 128], start=False, stop=True),
                        reads=[B_qkT[c], B_Rbf[n % 5]], writes=[B_P[bank]])
                for hh in range(H):
                    sc.op("act", lambda e, hh=hh: e.activation(
                        out=junk[:, hh * 128:(hh + 1) * 128], in_=psum[bank][:, hh * 128:(hh + 1) * 128],
                        func=AF.Square, accum_out=ossq[:, p2, hh:hh + 1]),
                        reads=[B_P[bank]], writes=[B_ossq[p2][hh]])
                sc.op("act", lambda e: e.activation(out=ort[:, p2, :], in_=ossq[:, p2, :], func=AF.Sqrt, bias=eps_t[:],
                                                    scale=1.0 / 128),
                      reads=B_ossq[p2] + [B_eps], writes=[B_ort[p2]])
                sc.op("dve", lambda e: e.reciprocal(out=orinv[:, p2, :], in_=ort[:, p2, :]),
                      reads=[B_ort[p2]], writes=[B_orinv[p2]])
                for hh in range(H):
                    sc.op("dve", lambda e, hh=hh: e.scalar_tensor_tensor(
                        out=btok[:, c, hh * 128:(hh + 1) * 128], in0=psum[bank][:, hh * 128:(hh + 1) * 128],
                        scalar=orinv[:, p2, hh:hh + 1], in1=sgb[:, c, hh * 128:(hh + 1) * 128],
                        op0=ALU.mult, op1=ALU.mult),
                        reads=[B_P[bank], B_orinv[p2], B_sgb[c]], writes=[B_btok[c]])

            def stage_c_new(c):
                n = gn[c]
                bank = 2 + c % 2
                p2 = c % 2
                for hh in range(H):
                    sc.op("pe", lambda e, hh=hh: e.matmul(
                        out=psum[bank][:, hh * 128:(hh + 1) * 128], lhsT=PT[:, p2, hh * 128:(hh + 1) * 128],
                        rhs=v_tok[:, c, hh * 128:(hh + 1) * 128], start=True, stop=False),
                        reads=[B_PT[p2], B_v[c]], writes=[B_P[bank]])
                    sc.op("pe", lambda e, hh=hh: e.matmul(
                        out=psum[bank][:, hh * 128:(hh + 1) * 128], lhsT=qkT[:, c, 0, hh * 128:(hh + 1) * 128],
                        rhs=Rbf[:, n % 5, hh * 128:(hh + 1) * 128], start=False, stop=True),
                        reads=[B_qkT[c], B_Rbf[n % 5]], writes=[B_P[bank]])
                osq = rB[:, 1, :]
                sc.op("act", lambda e: e.activation(out=osq, in_=psum[bank][:], func=AF.Square),
                      reads=[B_P[bank]], writes=[B_rB[1]])
                sc.op("dve", lambda e: e.tensor_reduce(
                    out=ossq[:, p2, :], in_=osq.rearrange("p (h e) -> p h e", h=H), axis=mybir.AxisListType.X,
                    op=ALU.add),
                    reads=[B_rB[1]], writes=B_ossq[p2])
                sc.op("act", lambda e: e.activation(out=ort[:, p2, :], in_=ossq[:, p2, :], func=AF.Sqrt, bias=eps_t[:],
                                                    scale=1.0 / 128),
                      reads=B_ossq[p2] + [B_eps], writes=[B_ort[p2]])
                sc.op("dve", lambda e: e.reciprocal(out=orinv[:, p2, :], in_=ort[:, p2, :]),
                      reads=[B_ort[p2]], writes=[B_orinv[p2]])
                sc.op("dve", lambda e: e.tensor_tensor(
                    out=osq.rearrange("p (h e) -> p h e", h=H), in0=psum[bank][:].rearrange("p (h e) -> p h e", h=H),
                    in1=orinv[:, p2, :].unsqueeze(2).broadcast_to([128, H, 128]), op=ALU.mult),
                    reads=[B_P[bank], B_orinv[p2]], writes=[B_rB[1]])
                sc.op("pool", lambda e: e.tensor_tensor(out=btok[:, c, :], in0=osq, in1=sgb[:, c, :], op=ALU.mult),
                      reads=[B_rB[1], B_sgb[c]], writes=[B_btok[c]])

            stage_c = stage_c_new if CMERGE else stage_c_old

            def stage_d(c, dbank=None):
                p2 = c % 2
                if dbank is None:
                    dbank = c % 2
                pbf = psum[dbank][:].bitcast(BF)
                for hh in range(H):
                    sc.op("pe", lambda e, hh=hh: e.transpose(
                        out=pbf[:, p2 * 512 + hh * 128: p2 * 512 + (hh + 1) * 128],
                        in_=btok[:, c, hh * 128:(hh + 1) * 128], identity=ident[:]),
                        reads=[B_btok[c], B_cbf], writes=[B_P[dbank]])
                sc.op("act", lambda e: e.activation(
                    out=mixT[:, 4:8, c * 128:(c + 1) * 128],
                    in_=pbf[:, p2 * 512:(p2 + 1) * 512].rearrange("p (h c) -> p h c", h=H), func=AF.Copy),
                    reads=[B_P[dbank]], writes=[B_act[4], B_act[5], B_act[6], B_act[7]])

            def pool1(g):
                bank = 7 if g % 2 == 0 else 4
                for c in range(NTT):
                    n = gn[c]
                    first_chunk = (n == 0)
                    bcur = bands[:, g * 3 + (2 if first_chunk else 0), :]
                    sc.op("pe", lambda e, c=c, bcur=bcur, fc=first_chunk, n=n: e.matmul(
                        out=psum[bank][:, c * 128:(c + 1) * 128], lhsT=up_tok[:, n % 5, g * 128:(g + 1) * 128],
                        rhs=bcur, start=True, stop=fc),
                        reads=[B_up[n % 5], B_cbf], writes=[B_P[bank]])
                    if not first_chunk:
                        sc.op("pe", lambda e, c=c, n=n: e.matmul(
                            out=psum[bank][:, c * 128:(c + 1) * 128],
                            lhsT=up_tok[:, (n - 1) % 5, g * 128:(g + 1) * 128],
                            rhs=bands[:, g * 3 + 1, :], start=False, stop=True),
                            reads=[B_up[(n - 1) % 5], B_cbf], writes=[B_P[bank]])
                sc.op("act", lambda e: e.activation(out=actT[:, 8 + g, :], in_=psum[bank][:], func=AF.Copy),
                      reads=[B_P[bank]], writes=[B_act[8 + g]])

            def pool2(g):
                bank2 = 6 + (g % 2)
                sc.op("pe", lambda e: e.matmul(
                    out=psum[bank2][:], lhsT=poolw_t[:, g, :], rhs=actT[:, 8 + g, :], start=True, stop=True),
                    reads=[B_cbf, B_act[8 + g]], writes=[B_P[bank2]])
                sc.op("act", lambda e: e.activation(
                    out=mixT[:, g, :], in_=psum[bank2][:], func=AF.Copy, scale=pscale_t[:, g:g + 1]),
                    reads=[B_P[bank2], B_const], writes=[B_act[g]])

            def wout_tt(tt):
                for dh in range(2):
                    bank = 4 + (tt % 2) * 2 + dh
                    for j in range(KC):
                        sl = wout_slots[2 * dh + j // 4]
                        wv = ring[sl].rearrange("p (k f) -> p k f", k=4)
                        sc.op("pe", lambda e, j=j, wv=wv, bank=bank: e.matmul(
                            out=psum[bank][:], lhsT=mixT[:, j, tt * 128:(tt + 1) * 128], rhs=wv[:, j % 4, :],
                            start=(j == 0), stop=(j == KC - 1)),
                            reads=[B_ring[sl], B_act[j]], writes=[B_P[bank]])
                for dh in range(2):
                    bank = 4 + (tt % 2) * 2 + dh
                    sc.op("dve", lambda e, dh=dh, bank=bank: e.tensor_tensor(
                        out=h[:, tt, dh * 512:(dh + 1) * 512], in0=psum[bank][:],
                        in1=h[:, tt, dh * 512:(dh + 1) * 512], op=ALU.add),
                        reads=[B_P[bank], B_h[hb][tt]], writes=[B_h[hb][tt]])

            def warm(n, bank):
                for _ in range(n):
                    sc.op("pe", lambda e: e.matmul(out=psum[bank][:], lhsT=ident[:],
                                                   rhs=bands[:].rearrange("p a b -> p (a b)")[:, 0:512],
                                                   start=True, stop=True),
                          reads=[B_cbf], writes=[B_P[bank]])

            proj(0)
            if tr3_pending:
                tr_tt(3, bank=7)
            proj(1)
            stage_a(0)
            proj(2)
            stage_a(1)
            if ORDER == 2:
                stage_b(0, bank=7)
                stage_b(1, bank=5)
                stage_c(0, bank=6)
            proj(3)
            for _ in range(10):
                issue_load()
            load_gain(2)
            stage_a(2)
            for g in range(4):
                pool1(g)
            if ORDER == 2:
                stage_a_tr(3)
                stage_b(2, bank=0)
                stage_c(1, bank=2)
                stage_b(3, bank=1)
                stage_c(2, bank=3)
                stage_d(0, dbank=0)
                for g in range(4):
                    pool2(g)
                stage_c(3, bank=2)
                stage_d(1, dbank=1)
                wout_tt(0)
                norm_tt(hb, 0)
                stage_d(2, dbank=0)
                wout_tt(1)
                norm_tt(hb, 1)
                stage_d(3, dbank=1)
                stage_a_kv(3, kvb=3)
                wout_tt(2)
                norm_tt(hb, 2)
                tr_tt(0)
                wout_tt(3)
                norm_tt(hb, 3)
                for _ in range(4):
                    issue_load()
                tr_tt(1)
                tr_tt(2, bank=2)
                tr_tt(3, bank=3)
                return
            stage_a(3)
            if ORDER == 0:
                stage_b(0)
                stage_b(1)
                stage_c(0)
                stage_b(2)
                stage_c(1)
                stage_d(0)
                stage_b(3)
                stage_c(2)
                stage_d(1)
                for g in range(4):
                    pool2(g)
                stage_c(3)
                stage_d(2)
                stage_d(3)
                for tt in range(NTT):
                    wout_tt(tt)
                    norm_tt(hb, tt)
                    if tt >= 1:
                        tr_tt(tt - 1)
                for _ in range(4):
                    issue_load()
                tr_tt(3)
            else:
                stage_b(0)
                stage_b(1)
                warm(WARM[0], 2)
                stage_c(0)
                stage_b(2)
                warm(WARM[1], 3)
                stage_c(1)
                stage_b(3)
                warm(WARM[2], 2)
                stage_c(2)
                stage_d(0)
                for g in range(4):
                    pool2(g)
                stage_c(3)
                stage_d(1)
                wout_tt(0)
                norm_tt(hb, 0)
                stage_d(2)
                wout_tt(1)
                norm_tt(hb, 1)
                stage_d(3)
                wout_tt(2)
                norm_tt(hb, 2)
                tr_tt(0)
                wout_tt(3)
                norm_tt(hb, 3)
                for _ in range(4):
                    issue_load()
                tr_tt(1)
                warm(WARM[3], 2)
                tr_tt(2, bank=2)
                tr_tt(3, bank=3)

        def emit_final(ti, hb, tts=range(NTT)):
            h = hbuf[hb]
            for tt in tts:
                i = 4 + tt
                sc.op("act", lambda e, tt=tt, i=i: e.activation(out=junk, in_=h[:, tt, :], func=AF.Square,
                                                              accum_out=ssq[:, i:i + 1]),
                      reads=[B_h[hb][tt]], writes=[B_ssq[i]])
                sc.op("act", lambda e, i=i: e.activation(out=rt[:, i:i + 1], in_=ssq[:, i:i + 1], func=AF.Sqrt,
                                                       bias=eps_t[:], scale=1.0 / D),
                      reads=[B_ssq[i], B_eps], writes=[B_rt[i]])
                sc.op("dve", lambda e, i=i: e.reciprocal(out=rstd[:, i:i + 1], in_=rt[:, i:i + 1]),
                      reads=[B_rt[i]], writes=[B_rstd[i]])
                sc.op("dve", lambda e, tt=tt, i=i: e.scalar_tensor_tensor(
                    out=h[:, tt, :], in0=h[:, tt, :], scalar=rstd[:, i:i + 1], in1=fgain_b[:],
                    op0=ALU.mult, op1=ALU.mult),
                    reads=[B_h[hb][tt], B_rstd[i], B_const], writes=[B_h[hb][tt]])
                store_y(ti, tt)

        load_x(0)
        load_rope(0)
        load_gain(0)
        for _ in range(NS):
            issue_load()
        for tt in range(NTT):
            norm_tt(0, tt)
        for tt in range(NTT):
            tr_tt(tt)
        for ti in range(ntiles):
            hb = ti % 2
            if ti + 1 < ntiles:
                load_x(ti + 1)
                load_rope(ti + 1)
            ffn_phase_a(ti, 0)
            load_gain(1)
            for tt in range(NTT):
                ffn_phase_b_tt(ti, hb, 0, tt)
                norm_tt(hb, tt)
                if tt >= 1:
                    tr_tt(tt - 1, bank=(4 if tt == 3 else None))
            for _ in range(NG):
                issue_load()
            mixer(ti, hb, True)
            ffn_phase_a(ti, 47)
            nxt = ti + 1 < ntiles
            if nxt:
                load_gain(0)
                for tt in range(NTT):
                    norm_tt(1 - hb, tt)
            for tt in range(NTT):
                ffn_phase_b_tt(ti, hb, 47, tt)
                if nxt and tt < 2:
                    tr_tt(2 * tt)
                    tr_tt(2 * tt + 1)
                emit_final(ti, hb, [tt])
            for _ in range(NG):
                issue_load()
        sc.wait_only("sp", out_ops)
        sc.wait_only("pool", out_ops)

        sc.finalize(engsem)
        with nc.Block() as block:
            @block.tensor
            def _(e):
                sc.emit("pe", e)

            @block.scalar
            def _(e):
                sc.emit("act", e)

            @block.vector
            def _(e):
                sc.emit("dve", e)

            @block.gpsimd
            def _(e):
                sc.emit("pool", e)

            @block.sync
            def _(e):
                sc.emit("sp", e)
    return nc


def make_consts():
    ident = np.eye(128, dtype=np.float32)
    windows = (2, 4, 8, 16)
    bands = np.zeros((128, 12, 128), np.float32)
    s = np.arange(128)[:, None]
    t = np.arange(128)[None, :]
    for g, w in enumerate(windows):
        cur = ((s <= t) & (s > t - w)).astype(np.float32) / w - (s == t).astype(np.float32)
        prev = ((s - 128 <= t) & (s - 128 > t - w)).astype(np.float32) / w
        cnt = np.minimum(t + 1, w).astype(np.float32)
        first = ((s <= t) & (s > t - w)).astype(np.float32) / cnt - (s == t).astype(np.float32)
        bands[:, g * 3 + 0, :] = cur
        bands[:, g * 3 + 1, :] = prev
        bands[:, g * 3 + 2, :] = first
    mask = (t >= s).astype(np.float32)
    hh = np.arange(H, dtype=np.float64)
    log_gamma = np.log1p(-np.exp2(-5.0 - hh))
    pos = np.arange(C, dtype=np.float64)
    qfac = np.exp(log_gamma[None, :] * (pos[:, None] + 1.0))
    kfac = np.exp(-log_gamma[None, :] * (pos[:, None] + 1.0)) * (128 ** -0.5)
    fac = np.concatenate([qfac, kfac], axis=1).astype(np.float32)
    gc = np.repeat(np.exp(log_gamma * C)[None, :, None], 128, axis=0)
    gc = np.repeat(gc, 128, axis=2).reshape(128, 512).astype(np.float32)
    inv_freq = 1.0 / (10000.0 ** (np.arange(0, 128, 2, dtype=np.float32) / np.float32(128)))
    ang = np.arange(S, dtype=np.float32)[:, None] * inv_freq[None, :].astype(np.float32)
    rope = np.concatenate([np.cos(ang), np.sin(ang)], axis=1).astype(np.float32)
    return dict(c_ident=ident, c_bands=bands, c_mask=mask, c_fac=fac, c_gc=gc, c_rope=rope)


_NC_CACHE = {}


def kernel(x, ffn1_norm, ffn1_w1, ffn1_w3, ffn1_w2, mix_norm, w_in, pool_w, pool_scale,
           ret_norm, w_out, ffn2_norm, ffn2_w1, ffn2_w3, ffn2_w2, final_norm, _ntiles=S // T):
    f = lambda a: np.ascontiguousarray(np.asarray(a, dtype=np.float32))
    x = f(x)
    B = x.shape[0]
    ntok = _ntiles * T
    shared = dict(
        w1a=f(ffn1_w1)[0], w3a=f(ffn1_w3)[0], w2a=f(ffn1_w2)[0],
        w1b=f(ffn2_w1)[0], w3b=f(ffn2_w3)[0], w2b=f(ffn2_w2)[0],
        w_in=f(w_in)[0], w_out=f(w_out)[0], pool_w=f(pool_w)[0],
        gains=np.ascontiguousarray(np.stack([f(ffn1_norm)[0], f(mix_norm)[0], f(ffn2_norm)[0]])),
        fgain=f(final_norm), retg=f(ret_norm)[0],
        pscale=np.ascontiguousarray(f(pool_scale)[0].reshape(4, 128).T),
    )
    shared.update(make_consts())
    if _ntiles not in _NC_CACHE:
        _NC_CACHE[_ntiles] = build_nc(_ntiles)
    nc = _NC_CACHE[_ntiles]
    in_maps = []
    for b in range(B):
        m = dict(shared)
        m["x"] = np.ascontiguousarray(x[b, :ntok])
        in_maps.append(m)
    res = run_bass_kernel_spmd(nc, in_maps, core_ids=list(range(B)))
    return np.stack([np.asarray(r["y"], dtype=np.float32) for r in res.results], axis=0)
```
